# Optimizing a Trainium2 kernel written in Bass

```python
import jax, jax.numpy as jnp
from jax import lax
import numpy as np

D_MODEL = 4096
BATCH = 1
SEQ = 8192
DEPTH = 1

CHUNK = 64
Q_BLOCK = 128
ROPE_THETA = 10000.0
NORM_EPS = 1e-6
ATTN_WIDTH = D_MODEL // 2
ATTN_HEAD_DIM = 128
ATTN_HEADS = ATTN_WIDTH // ATTN_HEAD_DIM
IDX_HEADS = 32
IDX_HEAD_DIM = 64
TOPK_MAX = 256
SSD_WIDTH = D_MODEL // 2
SSD_HEAD_DIM = 64
SSD_HEADS = SSD_WIDTH // SSD_HEAD_DIM
SSD_GROUPS = 8
SSD_STATE = 128
CONV_WIDTH = 4
CONV_CH = SSD_WIDTH + 2 * SSD_GROUPS * SSD_STATE
N_BRANCHES = 2
SPLIT_SIZES = (
    ATTN_WIDTH, ATTN_WIDTH, ATTN_WIDTH, ATTN_WIDTH,
    IDX_HEADS * IDX_HEAD_DIM, IDX_HEAD_DIM, IDX_HEADS,
    SSD_WIDTH, SSD_WIDTH, SSD_GROUPS * SSD_STATE, SSD_GROUPS * SSD_STATE, SSD_HEADS,
    N_BRANCHES * D_MODEL,
)
IN_WIDTH = sum(SPLIT_SIZES)

kernel_name = "hybrid_dsa_ssd_gated_merge_block"


def rms_norm(x, gain):
    xf = x.astype(jnp.float32)
    y = xf * lax.rsqrt(jnp.mean(xf * xf, axis=-1, keepdims=True) + NORM_EPS)
    return (y * gain.astype(jnp.float32)).astype(x.dtype)


def rope_tables(seq_len, dim):
    inv_freq = 1.0 / (ROPE_THETA ** (jnp.arange(0, dim, 2, dtype=jnp.float32) / dim))
    ang = jnp.arange(seq_len, dtype=jnp.float32)[:, None] * inv_freq[None, :]
    return jnp.cos(ang), jnp.sin(ang)


def apply_rope(x, cos, sin):
    xf = x.astype(jnp.float32)
    x1, x2 = jnp.split(xf, 2, axis=-1)
    c = cos[None, :, None, :]
    s = sin[None, :, None, :]
    return jnp.concatenate([x1 * c - x2 * s, x2 * c + x1 * s], axis=-1).astype(x.dtype)


def dsa_attention(q, k, v, qi, ki, wi):
    b, s = q.shape[0], q.shape[1]
    topk = min(TOPK_MAX, s // 4)
    nb = s // Q_BLOCK
    scale = ATTN_HEAD_DIM ** -0.5
    key_pos = jnp.arange(s)
    ki32 = ki.astype(jnp.float32)
    gather = jax.vmap(lambda table, idx: table[idx])

    def to_blocks(t):
        return jnp.moveaxis(t.reshape((b, nb, Q_BLOCK) + t.shape[2:]), 1, 0)

    def one_block(args):
        qb, qib, wib, blk = args
        q_pos = blk * Q_BLOCK + jnp.arange(Q_BLOCK)
        visible_end = (q_pos // CHUNK + 1) * CHUNK
        admissible = key_pos[None, :] < visible_end[:, None]
        idx_logits = jnp.einsum("bthd,bsd->bths", qib.astype(jnp.float32), ki32)
        idx_score = jnp.einsum("bths,bth->bts", jax.nn.relu(idx_logits), wib.astype(jnp.float32))
        idx_score = jnp.where(admissible[None], idx_score, -jnp.inf)
        _, sel = lax.top_k(idx_score, topk)
        sel_valid = sel < visible_end[None, :, None]
        k_sel = gather(k, sel).astype(jnp.float32)
        v_sel = gather(v, sel).astype(jnp.float32)
        logits = jnp.einsum("bthd,btkhd->bthk", qb.astype(jnp.float32), k_sel) * scale
        logits = jnp.where(sel_valid[:, :, None, :], logits, -jnp.inf)
        p = jax.nn.softmax(logits, axis=-1)
        return jnp.einsum("bthk,btkhd->bthd", p, v_sel).astype(q.dtype)

    out = lax.map(one_block, (to_blocks(q), to_blocks(qi), to_blocks(wi), jnp.arange(nb)))
    return jnp.moveaxis(out, 0, 1).reshape(q.shape)


def ssd_mixer(xs, z, bm, cm, dt_raw, conv_w, conv_b, dt_bias, a_log, d_skip, norm_gain):
    b, s, _ = xs.shape
    g, r, p, n = SSD_GROUPS, SSD_HEADS // SSD_GROUPS, SSD_HEAD_DIM, SSD_STATE
    nc = s // CHUNK
    xbc = jnp.concatenate([xs, bm, cm], axis=-1)
    xbc = lax.conv_general_dilated(
        xbc, conv_w[:, None, :].astype(xbc.dtype), window_strides=(1,),
        padding=[(CONV_WIDTH - 1, 0)], dimension_numbers=("NWC", "WIO", "NWC"),
        feature_group_count=CONV_CH) + conv_b
    xbc = jax.nn.silu(xbc)
    xs_c, bm_c, cm_c = jnp.split(xbc, [SSD_WIDTH, SSD_WIDTH + g * n], axis=-1)

    dt = jax.nn.softplus(dt_raw.astype(jnp.float32) + dt_bias.astype(jnp.float32))
    a = -jnp.exp(a_log.astype(jnp.float32))
    x = xs_c.astype(jnp.float32).reshape(b, nc, CHUNK, g, r, p)
    bc = bm_c.astype(jnp.float32).reshape(b, nc, CHUNK, g, n)
    cc = cm_c.astype(jnp.float32).reshape(b, nc, CHUNK, g, n)
    dtc = dt.reshape(b, nc, CHUNK, g, r)
    a_cum = jnp.cumsum(dtc * a.reshape(g, r), axis=2)
    xdt = x * dtc[..., None]

    a_t = jnp.moveaxis(a_cum, 2, -1)
    seg = a_t[..., :, None] - a_t[..., None, :]
    causal = jnp.tril(jnp.ones((CHUNK, CHUNK), dtype=bool))
    decay = jnp.exp(jnp.where(causal, seg, -jnp.inf))
    cb = jnp.einsum("bclgn,bcsgn->bcgls", cc, bc)
    y_diag = jnp.einsum("bcgrls,bcsgrp->bclgrp", cb[:, :, :, None] * decay, xdt)

    decay_to_end = jnp.exp(a_cum[:, :, -1:] - a_cum)
    states = jnp.einsum("bclgn,bclgrp->bcgrpn", bc, xdt * decay_to_end[..., None])
    chunk_decay = jnp.exp(a_cum[:, :, -1])

    def step(h, inp):
        st, dc = inp
        return h * dc[..., None, None] + st, h

    h0 = jnp.zeros((b, g, r, p, n), jnp.float32)
    _, h_in = lax.scan(step, h0, (jnp.moveaxis(states, 1, 0), jnp.moveaxis(chunk_decay, 1, 0)))
    h_in = jnp.moveaxis(h_in, 0, 1)
    y_off = jnp.einsum("bclgn,bcgrpn->bclgrp", cc, h_in) * jnp.exp(a_cum)[..., None]

    y = y_diag + y_off + x * d_skip.astype(jnp.float32).reshape(g, r)[:, :, None]
    y = y.reshape(b, s, SSD_WIDTH)
    yg = (y * jax.nn.silu(z.astype(jnp.float32))).reshape(b, s, g, SSD_WIDTH // g)
    yg = yg * lax.rsqrt(jnp.mean(yg * yg, axis=-1, keepdims=True) + NORM_EPS)
    return (yg.reshape(b, s, SSD_WIDTH) * norm_gain.astype(jnp.float32)).astype(xs.dtype)


def setup_inputs(seed: int = 0) -> dict:
    key = jax.random.key(seed)
    ks = jax.random.split(key, 13)
    f32 = jnp.float32
    x = jax.random.normal(ks[0], (BATCH, SEQ, D_MODEL), f32)
    pre_norm_gain = 1.0 + 0.05 * jax.random.normal(ks[1], (DEPTH, D_MODEL), f32)
    w_in = jax.random.normal(ks[2], (DEPTH, D_MODEL, IN_WIDTH), f32) * D_MODEL ** -0.5
    conv_w = jax.random.normal(ks[3], (DEPTH, CONV_WIDTH, CONV_CH), f32) * CONV_WIDTH ** -0.5
    conv_b = 0.01 * jax.random.normal(ks[4], (DEPTH, CONV_CH), f32)
    u = jax.random.uniform(ks[5], (DEPTH, SSD_HEADS), f32)
    dt0 = jnp.exp(u * (np.log(0.1) - np.log(0.001)) + np.log(0.001))
    dt_bias = dt0 + jnp.log(-jnp.expm1(-dt0))
    a_log = jnp.log(jax.random.uniform(ks[6], (DEPTH, SSD_HEADS), f32, 1.0, 16.0))
    d_skip = 1.0 + 0.1 * jax.random.normal(ks[7], (DEPTH, SSD_HEADS), f32)
    ssd_norm_gain = 1.0 + 0.05 * jax.random.normal(ks[8], (DEPTH, SSD_WIDTH), f32)
    w_branch_attn = jax.random.normal(ks[9], (DEPTH, ATTN_WIDTH, D_MODEL), f32) * ATTN_WIDTH ** -0.5
    w_branch_ssd = jax.random.normal(ks[10], (DEPTH, SSD_WIDTH, D_MODEL), f32) * SSD_WIDTH ** -0.5
    w_out = jax.random.normal(ks[11], (DEPTH, D_MODEL, D_MODEL), f32) * D_MODEL ** -0.5
    post_norm_gain = 1.0 + 0.05 * jax.random.normal(ks[12], (DEPTH, D_MODEL), f32)
    return {"x": x, "pre_norm_gain": pre_norm_gain, "w_in": w_in, "conv_w": conv_w,
            "conv_b": conv_b, "dt_bias": dt_bias, "a_log": a_log, "d_skip": d_skip,
            "ssd_norm_gain": ssd_norm_gain, "w_branch_attn": w_branch_attn,
            "w_branch_ssd": w_branch_ssd, "w_out": w_out, "post_norm_gain": post_norm_gain}


def reference(x, pre_norm_gain, w_in, conv_w, conv_b, dt_bias, a_log, d_skip,
              ssd_norm_gain, w_branch_attn, w_branch_ssd, w_out, post_norm_gain):
    b, s, _ = x.shape
    cos_a, sin_a = rope_tables(s, ATTN_HEAD_DIM)
    cos_i, sin_i = rope_tables(s, IDX_HEAD_DIM)
    split_points = [int(c) for c in np.cumsum(SPLIT_SIZES)[:-1]]
    idx_weight_scale = IDX_HEADS ** -0.5 * IDX_HEAD_DIM ** -0.5
    for layer in range(DEPTH):
        h = rms_norm(x, pre_norm_gain[layer])
        proj = h @ w_in[layer]
        (q, k, v, gate_a, qi, ki, wi, z, xs, bm, cm, dt_raw, merge_logits) = jnp.split(
            proj, split_points, axis=-1)

        q = apply_rope(q.reshape(b, s, ATTN_HEADS, ATTN_HEAD_DIM), cos_a, sin_a)
        k = apply_rope(k.reshape(b, s, ATTN_HEADS, ATTN_HEAD_DIM), cos_a, sin_a)
        v = v.reshape(b, s, ATTN_HEADS, ATTN_HEAD_DIM)
        qi = apply_rope(qi.reshape(b, s, IDX_HEADS, IDX_HEAD_DIM), cos_i, sin_i)
        ki = apply_rope(ki[:, :, None, :], cos_i, sin_i)[:, :, 0, :]
        attn = dsa_attention(q, k, v, qi, ki, wi * idx_weight_scale)
        y_a = attn.reshape(b, s, ATTN_WIDTH) * jax.nn.silu(gate_a)

        y_b = ssd_mixer(xs, z, bm, cm, dt_raw, conv_w[layer], conv_b[layer], dt_bias[layer],
                        a_log[layer], d_skip[layer], ssd_norm_gain[layer])

        gates = jax.nn.sigmoid(merge_logits.astype(jnp.float32)).reshape(b, s, N_BRANCHES, D_MODEL)
        merged = (gates[:, :, 0, :] * (y_a @ w_branch_attn[layer]).astype(jnp.float32)
                  + gates[:, :, 1, :] * (y_b @ w_branch_ssd[layer]).astype(jnp.float32))
        out = merged.astype(x.dtype) @ w_out[layer]
        x = x + rms_norm(out, post_norm_gain[layer])
    return x
```

```python
import contextlib
import numpy as np
import concourse.bass as bass
import concourse.mybir as mybir
from concourse.bass_utils import run_bass_kernel_spmd

F32 = mybir.dt.float32
BF16 = mybir.dt.bfloat16
AF = mybir.ActivationFunctionType
ALU = mybir.AluOpType
AX = mybir.AxisListType

CFG_FULL = dict(D=4096, S=8192, NCORE=8, AW=2048, IH=32, SW=2048, G=8, TOPK=256)
NEG = -1.0e30


def derive(cfg):
    c = dict(cfg)
    c["T"] = c["S"] // c["NCORE"]
    c["KC"] = c["D"] // 128
    c["NH"] = c["AW"] // 128
    c["SH"] = c["SW"] // 64
    c["R"] = c["SH"] // c["G"]
    c["GN"] = c["G"] * 128
    AW, IH, SW, GN, SH, D = c["AW"], c["IH"], c["SW"], c["GN"], c["SH"], c["D"]
    offs = {}
    o = 0
    for name, w in (("q", AW), ("k", AW), ("v", AW), ("ga", AW), ("qi", IH * 64), ("ki", 64), ("wi", IH),
                    ("z", SW), ("xs", SW), ("B", GN), ("C", GN), ("dt", SH), ("mg", 2 * D)):
        offs[name] = o
        o += w
    c["offs"] = offs
    c["INW"] = o
    return c


class Buf:
    def __init__(self, ap, name):
        self.ap = ap
        self.name = name
        self.w = {}
        self.r = {}
        self.dsem = None
        self.dcnt = 0


class Sched:
    def __init__(self, nc, es):
        self.nc = nc
        self.es = es
        self.engs = {"pe": nc.tensor, "act": nc.scalar, "dve": nc.vector, "pool": nc.gpsimd, "sp": nc.sync}
        self.sem = {e: es.enter_context(nc.semaphore("sem_" + e)) for e in ("pe", "act", "dve", "pool")}
        self.cnt = {e: 0 for e in self.sem}
        self.seen = {e: {} for e in self.engs}
        self.pending_pe = False
        self.all_dma = {}

    def _wait(self, e, deps):
        best = {}
        for (sem, val, key) in deps:
            if key == "pe" and e == "pe":
                continue
            k = id(sem)
            if k not in best or best[k][1] < val:
                best[k] = (sem, val)
        for k, (sem, val) in best.items():
            if self.seen[e].get(k, 0) >= val:
                continue
            self.engs[e].wait_ge(sem, val)
            self.seen[e][k] = val

    @staticmethod
    def _merge(d, t):
        k = id(t[0])
        if k not in d or d[k][1] < t[1]:
            d[k] = t

    def _deps(self, reads, writes, disjoint):
        deps = []
        for b in reads:
            deps += list(b.w.values())
        for b in writes:
            if not disjoint:
                deps += list(b.w.values())
            deps += list(b.r.values())
        return deps

    def op(self, e, fn, reads=(), writes=(), inc=True):
        self._wait(e, self._deps(reads, writes, False))
        ins = fn(self.engs[e])
        if inc:
            self.cnt[e] += 1
            ins.then_inc(self.sem[e], 1)
            t = (self.sem[e], self.cnt[e], e)
        else:
            assert e == "pe"
            t = (self.sem[e], self.cnt[e] + 1, e)
        for b in reads:
            self._merge(b.r, t)
        for b in writes:
            self._merge(b.w, t)
        return t

    def dma(self, e, pairs, reads, writes, sb, disjoint=False):
        if sb.dsem is None:
            sb.dsem = self.es.enter_context(self.nc.semaphore("d_" + sb.name))
        self._wait(e, self._deps(reads, writes, disjoint))
        for (o, i) in pairs:
            ins = self.engs[e].dma_start(out=o, in_=i)
            sb.dcnt += 16
            ins.then_inc(sb.dsem, 16)
        t = (sb.dsem, sb.dcnt, "dma_" + sb.name)
        self.all_dma[id(sb.dsem)] = t
        for b in reads:
            self._merge(b.r, t)
        for b in writes:
            self._merge(b.w, t)
        return t

    def barrier(self):
        ts = [(self.sem[e], self.cnt[e], e) for e in self.sem if self.cnt[e] > 0] + list(self.all_dma.values())
        for e in self.engs:
            self._wait(e, [t for t in ts if not (t[2] == "pe" and e == "pe")])


def build(cfg):
    c = derive(cfg)
    D, S, T, KC, AW, NH, IH, SW, SH, G, R, GN, TOPK = (c[k] for k in
        ("D", "S", "T", "KC", "AW", "NH", "IH", "SW", "SH", "G", "R", "GN", "TOPK"))
    offs, INW = c["offs"], c["INW"]
    NT, NG = S // 128, S // 512
    OT, OG = T // 128, T // 512
    OT0, OG0 = NT - OT, NG - OG
    NQG = T // 256
    EPS = 1e-6
    nc = bass.Bass("TRN2", target_bir_lowering=False)

    def din(name, shape, dt=F32):
        return nc.dram_tensor(name, list(shape), dt, kind="ExternalInput").ap()

    def dscr(name, shape, dt):
        return Buf(nc.dram_tensor(name, list(shape), dt, kind="Internal").ap(), name)

    x_loc = din("x_loc", [S, D])
    w_in = din("w_in", [D, INW])
    consts = din("consts", [128, 7 * 128])
    ropeA = din("ropeA", [2, 128, S])
    ropeI = din("ropeI", [2, 128, S])
    kmask = din("kmask", [1, S])
    valid_tm = din("valid_tm", [128, NT])
    convw = din("convw", [128, (SW + 2 * GN) // 128, 4])
    convb = din("convb", [128, (SW + 2 * GN) // 128])
    rows = din("rows", [1, 2 * D + SW + 3 * SH])
    w_ba = din("w_ba", [AW, D])
    w_bs = din("w_bs", [SW, D])
    w_o = din("w_o", [D, D])
    out_d = nc.dram_tensor("out", [T, D], F32, kind="ExternalOutput").ap()

    hT_d = dscr("hT_d", [NG, 128, KC, 512], BF16)
    kT_d = dscr("kT_d", [NH, 128, S], BF16)
    V_d = dscr("V_d", [NT, 128, AW], BF16)
    kiT_d = dscr("kiT_d", [128, S], BF16)
    xs_d = dscr("xs_d", [NT, 128, SW], BF16)
    B_d = dscr("B_d", [NT, 128, GN], BF16)
    BT_d = dscr("BT_d", [G, 128, S], BF16)
    CT_d = dscr("CT_d", [G, 128, S], BF16)
    dt_d = dscr("dt_d", [NT, 128, SH], F32)
    qT_d = dscr("qT_d", [NH, 128, T], BF16)
    gaT_d = dscr("gaT_d", [NH, 128, T], BF16)
    qiT_d = dscr("qiT_d", [IH // 2, 128, T], BF16)
    wi_d = dscr("wi_d", [OT, 128, IH], F32)
    zs_d = dscr("zs_d", [OT, 128, SW], BF16)
    gates_d = dscr("gates_d", [OT, 128, 2 * D], BF16)
    yaT_d = dscr("yaT_d", [NH, 128, T], BF16)
    ybT_d = dscr("ybT_d", [OT, 128, SW], BF16)
    mT_d = dscr("mT_d", [OT, 128, KC * 128], BF16)
    outp_d = dscr("outp_d", [OT, 128, D], F32)

    with contextlib.ExitStack() as es:
        Sc = Sched(nc, es)
        uid = [0]

        def sb(shape, dt, name, stack=None):
            uid[0] += 1
            nm = f"{name}_{uid[0]}"
            return Buf(((stack or es).enter_context(nc.sbuf_tensor(nm, list(shape), dt)))[:], nm)

        def ps(shape, dt, name, stack=None):
            uid[0] += 1
            nm = f"{name}_{uid[0]}"
            return Buf(((stack or es).enter_context(nc.psum_tensor(nm, list(shape), dt)))[:], nm)

        cst = sb([128, 7 * 128], F32, "cst")
        Sc.dma("sp", [(cst.ap, consts)], [], [cst], cst)
        ident_f, pswA_f, pswI_f, tri_f, negm_f, dmask_f, ones_f = (cst.ap[:, i * 128:(i + 1) * 128] for i in range(7))
        cbf = sb([128, 4 * 128], BF16, "cbf")
        Sc.op("dve", lambda e: e.tensor_copy(cbf.ap[:, 0:384], cst.ap[:, 0:384]), [cst], [cbf])
        Sc.op("dve", lambda e: e.tensor_copy(cbf.ap[:, 384:512], ones_f), [cst], [cbf])
        ident_b, pswA_b, pswI_b, ones_b = (cbf.ap[:, i * 128:(i + 1) * 128] for i in range(4))
        NROW = 2 * D + SW + 3 * SH
        o_ = 2 * D + SW
        rowbc = sb([128, 3 * SH], F32, "rowbc")
        Sc.dma("sp", [(rowbc.ap, rows[:, o_:o_ + 3 * SH].partition_broadcast(128))], [], [rowbc], rowbc)
        dtb_bc = rowbc.ap[:, 0:SH]
        alog_bc = rowbc.ap[:, SH:2 * SH]
        dsk_bc = rowbc.ap[:, 2 * SH:3 * SH]

        def load_row(c0, n, name, stack):
            b = sb([128, n], F32, name, stack)
            Sc.dma("sp", [(b.ap, rows[:, c0:c0 + n].partition_broadcast(128))], [], [b], b)
            return b
        a_bc = sb([128, SH], F32, "a_bc")
        Sc.op("act", lambda e: e.activation(out=a_bc.ap, in_=alog_bc, func=AF.Exp), [rowbc], [a_bc])
        Sc.op("dve", lambda e: e.tensor_scalar(a_bc.ap, a_bc.ap, -1.0, None, op0=ALU.mult), [a_bc], [a_bc])
        vtm = sb([128, NT], F32, "vtm")
        Sc.dma("sp", [(vtm.ap, valid_tm)], [], [vtm], vtm)
        NCB = (SW + 2 * GN) // 128
        cw = sb([128, NCB, 4], F32, "cw")
        cbias = sb([128, NCB], F32, "cbias")
        Sc.dma("sp", [(cw.ap, convw)], [], [cw], cw)
        Sc.dma("sp", [(cbias.ap, convb)], [], [cbias], cbias)

        psb = [ps([128, 512], F32, f"psb{i}") for i in range(4)]
        pst = [ps([128, 1024], BF16, f"pst{i}") for i in range(2)]
        pctr = [0, 0]

        def nps():
            pctr[0] += 1
            return psb[pctr[0] % 4]

        def npt():
            pctr[1] += 1
            return pst[pctr[1] % 2]

        def copy_eng(i):
            return "act"

        def cp(e, out_ap, in_ap, reads, writes):
            if e == "act":
                return Sc.op("act", lambda q: q.activation(out=out_ap, in_=in_ap, func=AF.Copy), reads, writes)
            return Sc.op(e, lambda q: q.tensor_copy(out_ap, in_ap), reads, writes)

        with contextlib.ExitStack() as st:
            pre_gb = load_row(0, D, "pre_g", st)
            pre_g = pre_gb.ap
            xt = [sb([128, D], F32, "xt", st) for _ in range(2)]
            hb = [sb([128, D], BF16, "hb", st) for _ in range(2)]
            junk = sb([128, D], BF16, "junkA", st)
            hTt = [sb([128, KC, 512], BF16, "hTt", st) for _ in range(2)]
            ssb = [sb([128, 4], F32, "ssA", st) for _ in range(2)]
            for g in range(NG):
                hg = hTt[g % 2]
                for tt in range(4):
                    t = g * 4 + tt
                    xb, h_, ss = xt[t % 2], hb[t % 2], ssb[t % 2]
                    Sc.dma("sp", [(xb.ap, x_loc[t * 128:(t + 1) * 128, :])], [], [xb], xb)
                    Sc.op("act", lambda e: e.activation(out=junk.ap, in_=xb.ap, func=AF.Square, accum_out=ss.ap[:, 0:1]), [xb], [junk, ss])
                    Sc.op("dve", lambda e: e.tensor_scalar(ss.ap[:, 1:2], ss.ap[:, 0:1], 1.0 / D, EPS, op0=ALU.mult, op1=ALU.add), [ss], [ss])
                    Sc.op("act", lambda e: e.activation(out=ss.ap[:, 2:3], in_=ss.ap[:, 1:2], func=AF.Sqrt), [ss], [ss])
                    Sc.op("dve", lambda e: e.reciprocal(ss.ap[:, 3:4], ss.ap[:, 2:3]), [ss], [ss])
                    Sc.op("dve", lambda e: e.scalar_tensor_tensor(out=h_.ap, in0=xb.ap, scalar=ss.ap[:, 3:4], in1=pre_g, op0=ALU.mult, op1=ALU.mult), [xb, ss, pre_gb], [h_])
                    for k8 in range(0, KC, 8):
                        nk = min(8, KC - k8)
                        pt = npt()
                        for j in range(nk):
                            Sc.op("pe", lambda e, j=j: e.transpose(pt.ap[:, j * 128:(j + 1) * 128], h_.ap[:, (k8 + j) * 128:(k8 + j + 1) * 128], ident_b), [h_, cbf], [pt], inc=(j == nk - 1))
                        cp(copy_eng(k8 // 8), hg.ap[:, k8:k8 + nk, tt * 128:(tt + 1) * 128], pt.ap[:, 0:nk * 128].rearrange("p (k n) -> p k n", n=128), [pt], [hg])
                Sc.dma("sp", [(hT_d.ap[g], hg.ap)], [hg], [hT_d], hg, disjoint=True)
        Sc.barrier()

        STOP = cfg.get('STOP', 99)
        if STOP < 2:
            return nc
        with contextlib.ExitStack() as st:
            Wb = [sb([128, KC, 512], BF16, "Wb", st) for _ in range(2)]
            Hb = [sb([128, KC, 512], BF16, "Hb", st) for _ in range(2)]
            rtab = [sb([128, 4, 512], F32, "rtab", st) for _ in range(2)]
            stg = [sb([128, 4, 128], BF16, "stg", st) for _ in range(3)]
            ob = [sb([128, 512], BF16, "ob", st) for _ in range(3)]
            of = [sb([128, 512], F32, "of", st) for _ in range(4)]
            sm = [sb([128, 64], F32, "sm", st) for _ in range(2)]
            ctr = {"w": 0, "h": 0, "o": 0, "f": 0, "s": 0, "r": 0, "m": 0}

            def nxt(lst, key):
                ctr[key] += 1
                return lst[ctr[key] % len(lst)]

            pj = [0]
            DBG = cfg.get('DBG', 0)
            STOPB = cfg.get('STOPB', 999)

            def project(segs, groups, mode, cbk, need_rope=False):
                pj[0] += 1
                if pj[0] > STOPB:
                    return
                ncols = sum(n for _, n in segs)
                W = nxt(Wb, "w")
                pairs = []
                o = 0
                for (c0, n) in segs:
                    pairs.append((W.ap[:, :, o:o + n], w_in[:, c0:c0 + n].rearrange("(k p) c -> p k c", p=128)))
                    o += n
                Sc.dma("pool", pairs, [], [W], W)
                pend = []
                for gi, g in enumerate(groups):
                    H = nxt(Hb, "h")
                    Sc.dma("sp", [(H.ap, hT_d.ap[g])], [hT_d], [H], H)
                    rt = None
                    if need_rope:
                        rt = nxt(rtab, "r")
                        Sc.dma("sp", [(rt.ap[:, 0:2, :], ropeA[:, :, g * 512:(g + 1) * 512].rearrange("a p s -> p a s")),
                                      (rt.ap[:, 2:4, :], ropeI[:, :, g * 512:(g + 1) * 512].rearrange("a p s -> p a s"))], [], [rt], rt)
                    if mode == "fm":
                        for sbk in range((ncols + 127) // 128):
                            m = min(128, ncols - sbk * 128)
                            p = nps()
                            for k in range(KC):
                                Sc.op("pe", lambda e, k=k: e.matmul(p.ap[:m, :], W.ap[:, k, sbk * 128:sbk * 128 + m], H.ap[:, k, :], start=(k == 0), stop=(k == KC - 1)), [W, H], [p], inc=(k == KC - 1))
                            if pend:
                                pend.pop()()
                            pend.append(lambda sbk=sbk, m=m, g=g, gi=gi, p=p, rt=rt: cbk(sbk, m, g, gi, p, rt))
                    else:
                        for tt in range(4):
                            p = nps()
                            for k in range(KC):
                                Sc.op("pe", lambda e, k=k: e.matmul(p.ap[:, :ncols], H.ap[:, k, tt * 128:(tt + 1) * 128], W.ap[:, k, :ncols], start=(k == 0), stop=(k == KC - 1)), [W, H], [p], inc=(k == KC - 1))
                            if pend:
                                pend.pop()()
                            pend.append(lambda t_=g * 4 + tt, p=p: cbk(t_, ncols, p))
                if pend:
                    pend.pop()()

            def rope_store(p, m, rt, ti, psw_b, dst_buf, dst_ap):
                pf = nxt(of, "f")
                Sc.op("act", lambda e: e.activation(out=pf.ap[:m, :], in_=p.ap[:m, :], func=AF.Copy), [p], [pf])
                raw = nxt(ob, "o")
                Sc.op("act", lambda e: e.activation(out=raw.ap[:m, :], in_=p.ap[:m, :], func=AF.Copy), [p], [raw])
                p2 = nps()
                Sc.op("pe", lambda e: e.matmul(p2.ap[:m, :], psw_b[:m, :m], raw.ap[:m, :], start=True, stop=True), [raw, cbf], [p2])
                f3 = nxt(of, "f")
                Sc.op("act", lambda e: e.activation(out=f3.ap[:m, :], in_=p2.ap[:m, :], func=AF.Copy), [p2], [f3])
                Sc.op("dve", lambda e: e.tensor_tensor(pf.ap[:m, :], pf.ap[:m, :], rt.ap[:m, ti, :], ALU.mult), [pf, rt], [pf])
                Sc.op("dve", lambda e: e.tensor_tensor(f3.ap[:m, :], f3.ap[:m, :], rt.ap[:m, ti + 1, :], ALU.mult), [f3, rt], [f3])
                o = nxt(ob, "o")
                Sc.op("dve", lambda e: e.tensor_tensor(o.ap[:m, :], pf.ap[:m, :], f3.ap[:m, :], ALU.add), [pf, f3], [o])
                Sc.dma("sp", [(dst_ap, o.ap[:m, :])], [o], [dst_buf], o, disjoint=True)

            ALLG = list(range(NG))
            OWNG = list(range(OG0, NG))
            for cbk4 in range(AW // 512):
                def k_cb(sbk, m, g, gi, p, rt, cbk4=cbk4):
                    hd = cbk4 * 4 + sbk
                    rope_store(p, m, rt, 0, pswA_b, kT_d, kT_d.ap[hd][:, g * 512:(g + 1) * 512])
                project([(offs["k"] + cbk4 * 512, 512)], ALLG, "fm", k_cb, need_rope=True)
            for cbk4 in range(AW // 512):
                def v_cb(t, ncols, p, cbk4=cbk4):
                    o = nxt(ob, "o")
                    cp("act", o.ap, p.ap, [p], [o])
                    Sc.dma("sp", [(V_d.ap[t][:, cbk4 * 512:(cbk4 + 1) * 512], o.ap)], [o], [V_d], o, disjoint=True)
                project([(offs["v"] + cbk4 * 512, 512)], ALLG, "tm", v_cb)
            def ki_cb(sbk, m, g, gi, p, rt):
                rope_store(p, m, rt, 2, pswI_b, kiT_d, kiT_d.ap[:, g * 512:(g + 1) * 512])
            project([(offs["ki"], 64), (offs["ki"], 64)], ALLG, "fm", ki_cb, need_rope=True)

            conv_R = [[sb([128, 515], F32, "cR", st) for _ in range(2)] for _ in range(4)]

            def conv_block(col0, cblk0, nblk, groups, store):
                def cv_cb(sbk, m, g, gi, p, rt):
                    cbi = cblk0 + sbk
                    Rcur = conv_R[sbk][gi % 2]
                    Rprev = conv_R[sbk][(gi + 1) % 2]
                    if gi == 0:
                        Sc.op("pool", lambda e: e.memset(Rcur.ap[:, 0:3], 0.0), [], [Rcur])
                    else:
                        Sc.op("pool", lambda e: e.tensor_copy(Rcur.ap[:, 0:3], Rprev.ap[:, 512:515]), [Rprev], [Rcur])
                    Sc.op("act", lambda e: e.activation(out=Rcur.ap[:, 3:515], in_=p.ap, func=AF.Copy), [p], [Rcur])
                    acc = nxt(of, "f")
                    Sc.op("act", lambda e: e.activation(out=acc.ap, in_=Rcur.ap[:, 3:515], func=AF.Identity, bias=cbias.ap[:, cbi:cbi + 1], scale=cw.ap[:, cbi, 3:4]), [Rcur, cbias, cw], [acc])
                    for j in range(3):
                        Sc.op("dve", lambda e, j=j: e.scalar_tensor_tensor(out=acc.ap, in0=Rcur.ap[:, j:j + 512], scalar=cw.ap[:, cbi, j:j + 1], in1=acc.ap, op0=ALU.mult, op1=ALU.add), [Rcur, cw, acc], [acc])
                    o = nxt(ob, "o")
                    Sc.op("act", lambda e: e.activation(out=o.ap, in_=acc.ap, func=AF.Silu), [acc], [o])
                    store(sbk, g, gi, o)
                project([(col0, nblk * 128)], groups, "fm", cv_cb)

            def to_tm_store(o, g, dst_buf, dst_fn):
                pt = npt()
                for tt in range(4):
                    Sc.op("pe", lambda e, tt=tt: e.transpose(pt.ap[:, tt * 128:(tt + 1) * 128], o.ap[:, tt * 128:(tt + 1) * 128], ident_b), [o, cbf], [pt], inc=(tt == 3))
                s_ = nxt(stg, "s")
                cp("act", s_.ap, pt.ap[:, 0:512].rearrange("p (t n) -> p t n", n=128), [pt], [s_])
                Sc.dma("sp", [(dst_fn(g), s_.ap)], [s_], [dst_buf], s_, disjoint=True)

            if True:
                for cb4 in range(SW // 512):
                    def xs_store(sbk, g, gi, o, cb4=cb4):
                        c0 = cb4 * 512 + sbk * 128
                        to_tm_store(o, g, xs_d, lambda g: xs_d.ap[g * 4:(g + 1) * 4, :, c0:c0 + 128].rearrange("t p c -> p t c"))
                    conv_block(offs["xs"] + cb4 * 512, cb4 * 4, 4, ALLG, xs_store)
                for cb4 in range(GN // 512):
                    def b_store(sbk, g, gi, o, cb4=cb4):
                        gg = cb4 * 4 + sbk
                        Sc.dma("sp", [(BT_d.ap[gg][:, g * 512:(g + 1) * 512], o.ap)], [o], [BT_d], o, disjoint=True)
                        to_tm_store(o, g, B_d, lambda g: B_d.ap[g * 4:(g + 1) * 4, :, gg * 128:gg * 128 + 128].rearrange("t p c -> p t c"))
                    conv_block(offs["B"] + cb4 * 512, SW // 128 + cb4 * 4, min(4, GN // 128), ALLG, b_store)
                for cb4 in range(GN // 512):
                    def c_store(sbk, g, gi, o, cb4=cb4):
                        gg = cb4 * 4 + sbk
                        Sc.dma("sp", [(CT_d.ap[gg][:, g * 512:(g + 1) * 512], o.ap)], [o], [CT_d], o, disjoint=True)
                    conv_block(offs["C"] + cb4 * 512, (SW + GN) // 128 + cb4 * 4, min(4, GN // 128), [OG0 - 1] + OWNG, c_store)

            def dt_cb(t, ncols, p):
                s_ = nxt(sm, "m")
                Sc.op("act", lambda e: e.activation(out=s_.ap[:, 0:SH], in_=p.ap[:, 0:SH], func=AF.Copy), [p], [s_])
                Sc.op("dve", lambda e: e.tensor_tensor(s_.ap[:, 0:SH], s_.ap[:, 0:SH], dtb_bc, ALU.add), [s_, rowbc], [s_])
                Sc.op("act", lambda e: e.activation(out=s_.ap[:, 0:SH], in_=s_.ap[:, 0:SH], func=AF.Exp), [s_], [s_])
                Sc.op("act", lambda e: e.activation(out=s_.ap[:, 0:SH], in_=s_.ap[:, 0:SH], func=AF.Ln, bias=1.0), [s_], [s_])
                Sc.op("dve", lambda e: e.tensor_scalar(s_.ap[:, 0:SH], s_.ap[:, 0:SH], vtm.ap[:, t:t + 1], None, op0=ALU.mult), [s_, vtm], [s_])
                Sc.dma("sp", [(dt_d.ap[t], s_.ap[:, 0:SH])], [s_], [dt_d], s_, disjoint=True)
            project([(offs["dt"], SH)], ALLG, "tm", dt_cb)

            for cbk4 in range(AW // 512):
                def q_cb(sbk, m, g, gi, p, rt, cbk4=cbk4):
                    hd = cbk4 * 4 + sbk
                    rope_store(p, m, rt, 0, pswA_b, qT_d, qT_d.ap[hd][:, gi * 512:(gi + 1) * 512])
                project([(offs["q"] + cbk4 * 512, 512)], OWNG, "fm", q_cb, need_rope=True)
            for cbk4 in range(AW // 512):
                def ga_cb(sbk, m, g, gi, p, rt, cbk4=cbk4):
                    hd = cbk4 * 4 + sbk
                    o = nxt(ob, "o")
                    Sc.op("act", lambda e: e.activation(out=o.ap, in_=p.ap, func=AF.Silu), [p], [o])
                    Sc.dma("sp", [(gaT_d.ap[hd][:, gi * 512:(gi + 1) * 512], o.ap)], [o], [gaT_d], o, disjoint=True)
                project([(offs["ga"] + cbk4 * 512, 512)], OWNG, "fm", ga_cb)
            for cbk4 in range(IH * 64 // 512):
                def qi_cb(sbk, m, g, gi, p, rt, cbk4=cbk4):
                    hp = cbk4 * 4 + sbk
                    rope_store(p, m, rt, 2, pswI_b, qiT_d, qiT_d.ap[hp][:, gi * 512:(gi + 1) * 512])
                project([(offs["qi"] + cbk4 * 512, min(512, IH * 64))], OWNG, "fm", qi_cb, need_rope=True)
            wscale = float(IH ** -0.5 * 64 ** -0.5)

            def wi_cb(t, ncols, p):
                s_ = nxt(sm, "m")
                Sc.op("act", lambda e: e.activation(out=s_.ap[:, 0:IH], in_=p.ap[:, 0:IH], func=AF.Copy, scale=wscale), [p], [s_])
                Sc.dma("sp", [(wi_d.ap[t - OT0], s_.ap[:, 0:IH])], [s_], [wi_d], s_, disjoint=True)
            project([(offs["wi"], IH)], OWNG, "tm", wi_cb)
            for cb4 in range(SW // 512):
                def z_cb(t, ncols, p, cb4=cb4):
                    o = nxt(ob, "o")
                    Sc.op("act", lambda e: e.activation(out=o.ap, in_=p.ap, func=AF.Silu), [p], [o])
                    Sc.dma("sp", [(zs_d.ap[t - OT0][:, cb4 * 512:(cb4 + 1) * 512], o.ap)], [o], [zs_d], o, disjoint=True)
                project([(offs["z"] + cb4 * 512, 512)], OWNG, "tm", z_cb)
            for cb4 in range(2 * D // 512):
                def mg_cb(t, ncols, p, cb4=cb4):
                    o = nxt(ob, "o")
                    Sc.op("act", lambda e: e.activation(out=o.ap, in_=p.ap, func=AF.Sigmoid), [p], [o])
                    Sc.dma("sp", [(gates_d.ap[t - OT0][:, cb4 * 512:(cb4 + 1) * 512], o.ap)], [o], [gates_d], o, disjoint=True)
                project([(offs["mg"] + cb4 * 512, 512)], OWNG, "tm", mg_cb)
        Sc.barrier()

        if STOP < 3:
            return nc
        with contextlib.ExitStack() as st:
            ngb = load_row(2 * D, SW, "ngain", st)
            ngain = ngb.ap
            hst = sb([128, SH, 64], F32, "hst", st)
            Sc.op("pool", lambda e: e.memset(hst.ap, 0.0), [], [hst])
            xsb = [sb([128, SH, 64], BF16, "xsb", st) for _ in range(2)]
            Bb = [sb([128, GN], BF16, "Bb", st) for _ in range(2)]
            dtb = [sb([128, SH], F32, "dtb", st) for _ in range(2)]
            sc_ = [sb([128, 8, SH], F32, "sc", st) for _ in range(2)]
            xdtd = [sb([128, SH, 64], BF16, "xdtd", st) for _ in range(2)]
            xdt = sb([128, SH, 64], BF16, "xdt", st)
            hbf = sb([128, SH, 64], BF16, "hbf", st)
            Rm = sb([128, SH, 128], F32, "Rm", st)
            E4 = [sb([128, 4, 128], F32, "E4", st) for _ in range(2)]
            dec = [sb([128, 128], F32, "dec", st) for _ in range(2)]
            ea = [sb([128, 128], F32, "ea", st) for _ in range(2)]
            MT = [sb([128, 128], BF16, "MT", st) for _ in range(2)]
            CsT = [sb([128, 128], BF16, "CsT", st) for _ in range(2)]
            cbT = [sb([128, 128], F32, "cbT", st) for _ in range(G)]
            BTt = sb([128, G, 128], BF16, "BTt", st)
            CTt = sb([128, G, 128], BF16, "CTt", st)
            ysb = sb([128, SH, 64], F32, "ysb", st)
            tmpy = sb([128, 4, 64], F32, "tmpy", st)
            zsb = sb([128, SW], BF16, "zsb", st)
            ybn = sb([128, SW], BF16, "ybn", st)
            junkc = sb([128, SW // G], BF16, "junkc", st)
            ssq = sb([128, 3 * G], F32, "ssq", st)
            ybT = sb([128, SW // 128, 128], BF16, "ybTt", st)
            pSsb = [sb([128, 512], F32, "pSs", st) for _ in range(2)]
            for t in range(NT):
                X, Bt, dtt, s_ = xsb[t % 2], Bb[t % 2], dtb[t % 2], sc_[t % 2]
                Sc.dma("sp", [(X.ap.rearrange("p h d -> p (h d)"), xs_d.ap[t])], [xs_d], [X], X)
                Sc.dma("sp", [(Bt.ap, B_d.ap[t])], [B_d], [Bt], Bt)
                Sc.dma("sp", [(dtt.ap, dt_d.ap[t])], [dt_d], [dtt], dtt)
                dta, acum, dte, cd, wgt, nac, aend = (s_.ap[:, i, :] for i in range(7))
                Sc.op("dve", lambda e: e.tensor_tensor(dta, dtt.ap, a_bc.ap, ALU.mult), [dtt, a_bc], [s_])
                pa, pe_ = nps(), nps()
                Sc.op("pe", lambda e: e.matmul(pa.ap[:, 0:SH], tri_f, dta, start=True, stop=True), [cst, s_], [pa])
                Sc.op("pe", lambda e: e.matmul(pe_.ap[:, 0:SH], ones_f, dta, start=True, stop=True), [cst, s_], [pe_])
                Sc.op("act", lambda e: e.activation(out=acum, in_=pa.ap[:, 0:SH], func=AF.Copy), [pa], [s_])
                Sc.op("act", lambda e: e.activation(out=aend, in_=pe_.ap[:, 0:SH], func=AF.Copy), [pe_], [s_])
                Sc.op("dve", lambda e: e.tensor_tensor(dte, aend, acum, ALU.subtract), [s_], [s_])
                Sc.op("act", lambda e: e.activation(out=dte, in_=dte, func=AF.Exp), [s_], [s_])
                Sc.op("act", lambda e: e.activation(out=cd, in_=pe_.ap[:, 0:SH], func=AF.Exp), [pe_], [s_])
                Sc.op("dve", lambda e: e.tensor_tensor(wgt, dtt.ap, dte, ALU.mult), [dtt, s_], [s_])
                XD = xdtd[t % 2]
                Sc.op("pool", lambda e: e.tensor_tensor(XD.ap, X.ap, wgt.unsqueeze(2).broadcast_to([128, SH, 64]), ALU.mult), [X, s_], [XD])
                if t >= OT0:
                    to = t - OT0
                    Sc.op("dve", lambda e: e.tensor_scalar(nac, acum, -1.0, None, op0=ALU.mult), [s_], [s_])
                    Sc.op("pool", lambda e: e.tensor_tensor(xdt.ap, X.ap, dtt.ap.unsqueeze(2).broadcast_to([128, SH, 64]), ALU.mult), [X, dtt], [xdt])
                    Sc.op("act", lambda e: e.activation(out=hbf.ap, in_=hst.ap, func=AF.Copy), [hst], [hbf])
                    Sc.dma("sp", [(BTt.ap, BT_d.ap[:, :, t * 128:(t + 1) * 128].rearrange("g p s -> p g s"))], [BT_d], [BTt], BTt)
                    Sc.dma("sp", [(CTt.ap, CT_d.ap[:, :, t * 128:(t + 1) * 128].rearrange("g p s -> p g s"))], [CT_d], [CTt], CTt)
                    Sc.dma("sp", [(zsb.ap, zs_d.ap[to])], [zs_d], [zsb], zsb)
                    for g in range(G):
                        pc = nps()
                        Sc.op("pe", lambda e, g=g: e.matmul(pc.ap[:, 0:128], BTt.ap[:, g, :], CTt.ap[:, g, :], start=True, stop=True), [BTt, CTt], [pc])
                        cp("act", cbT[g].ap, pc.ap[:, 0:128], [pc], [cbT[g]])
                    Sc.op("pool", lambda e: e.tensor_tensor(Rm.ap, tri_f.unsqueeze(1).broadcast_to([128, SH, 128]), dta.unsqueeze(2).broadcast_to([128, SH, 128]), ALU.mult), [cst, s_], [Rm])
                    for hg in range(SH // 4):
                        pbc = nps()
                        Sc.op("pe", lambda e: e.matmul(pbc.ap, ones_f, Rm.ap[:, hg * 4:(hg + 1) * 4, :].rearrange("p h l -> p (h l)"), start=True, stop=True), [cst, Rm], [pbc])
                        e4 = E4[hg % 2]
                        Sc.op("act", lambda e: e.activation(out=e4.ap.rearrange("p h l -> p (h l)"), in_=pbc.ap, func=AF.Copy), [pbc], [e4])
                        Sc.op("dve", lambda e: e.tensor_tensor(e4.ap, e4.ap, negm_f.unsqueeze(1).broadcast_to([128, 4, 128]), ALU.add), [e4, cst], [e4])
                        py = nps()
                        for j in range(4):
                            hh = hg * 4 + j
                            g = hh // R
                            d_, ea_, mt_, cs_ = dec[j % 2], ea[j % 2], MT[j % 2], CsT[j % 2]
                            Sc.op("act", lambda e: e.activation(out=d_.ap, in_=e4.ap[:, j, :], func=AF.Exp, bias=nac[:, hh:hh + 1]), [e4, s_], [d_])
                            Sc.op("dve", lambda e: e.tensor_tensor(mt_.ap, cbT[g].ap, d_.ap, ALU.mult), [cbT[g], d_], [mt_])
                            Sc.op("act", lambda e: e.activation(out=ea_.ap, in_=pbc.ap[:, j * 128:(j + 1) * 128], func=AF.Exp), [pbc], [ea_])
                            Sc.op("dve", lambda e: e.tensor_tensor(cs_.ap, CTt.ap[:, g, :], ea_.ap, ALU.mult), [CTt, ea_], [cs_])
                            Sc.op("pe", lambda e: e.matmul(py.ap[:, j * 64:(j + 1) * 64], mt_.ap, xdt.ap[:, hh, :], start=True, stop=False), [mt_, xdt], [py], inc=False)
                            Sc.op("pe", lambda e: e.matmul(py.ap[:, j * 64:(j + 1) * 64], cs_.ap, hbf.ap[:, hh, :], start=False, stop=True), [cs_, hbf], [py])
                        Sc.op("dve", lambda e: e.tensor_tensor(tmpy.ap, X.ap[:, hg * 4:(hg + 1) * 4, :], dsk_bc[:, hg * 4:(hg + 1) * 4].unsqueeze(2).broadcast_to([128, 4, 64]), ALU.mult), [X, rowbc], [tmpy])
                        Sc.op("act", lambda e: e.activation(out=ysb.ap[:, hg * 4:(hg + 1) * 4, :].rearrange("p h d -> p (h d)"), in_=py.ap[:, 0:256], func=AF.Copy), [py], [ysb])
                        Sc.op("dve", lambda e: e.tensor_tensor(ysb.ap[:, hg * 4:(hg + 1) * 4, :], ysb.ap[:, hg * 4:(hg + 1) * 4, :], tmpy.ap, ALU.add), [ysb, tmpy], [ysb])
                    yflat = ysb.ap.rearrange("p h d -> p (h d)")
                    Sc.op("dve", lambda e: e.tensor_tensor(yflat, yflat, zsb.ap, ALU.mult), [ysb, zsb], [ysb])
                    GW = SW // G
                    for g in range(G):
                        Sc.op("act", lambda e, g=g: e.activation(out=junkc.ap, in_=yflat[:, g * GW:(g + 1) * GW], func=AF.Square, accum_out=ssq.ap[:, g:g + 1]), [ysb], [junkc, ssq])
                    Sc.op("dve", lambda e: e.tensor_scalar(ssq.ap[:, G:2 * G], ssq.ap[:, 0:G], 1.0 / GW, EPS, op0=ALU.mult, op1=ALU.add), [ssq], [ssq])
                    Sc.op("act", lambda e: e.activation(out=ssq.ap[:, G:2 * G], in_=ssq.ap[:, G:2 * G], func=AF.Sqrt), [ssq], [ssq])
                    Sc.op("dve", lambda e: e.reciprocal(ssq.ap[:, 2 * G:3 * G], ssq.ap[:, G:2 * G]), [ssq], [ssq])
                    for g in range(G):
                        Sc.op("dve", lambda e, g=g: e.scalar_tensor_tensor(out=ybn.ap[:, g * GW:(g + 1) * GW], in0=yflat[:, g * GW:(g + 1) * GW], scalar=ssq.ap[:, 2 * G + g:2 * G + g + 1], in1=ngain[:, g * GW:(g + 1) * GW], op0=ALU.mult, op1=ALU.mult), [ysb, ssq, ngb], [ybn])
                    for k8 in range(0, SW // 128, 8):
                        nk = min(8, SW // 128 - k8)
                        pt = npt()
                        for j in range(nk):
                            Sc.op("pe", lambda e, j=j: e.transpose(pt.ap[:, j * 128:(j + 1) * 128], ybn.ap[:, (k8 + j) * 128:(k8 + j + 1) * 128], ident_b), [ybn, cbf], [pt], inc=(j == nk - 1))
                        cp("act", ybT.ap[:, k8:k8 + nk, :], pt.ap[:, 0:nk * 128].rearrange("p (k n) -> p k n", n=128), [pt], [ybT])
                    Sc.dma("sp", [(ybT_d.ap[to], ybT.ap.rearrange("p k n -> p (k n)"))], [ybT], [ybT_d], ybT, disjoint=True)
                hfl = hst.ap.rearrange("p h d -> p (h d)")
                Sc.op("dve", lambda e: e.tensor_tensor(hst.ap, hst.ap, cd.unsqueeze(2).broadcast_to([128, SH, 64]), ALU.mult), [hst, s_], [hst])
                for b in range((SW + 511) // 512):
                    w = min(512, SW - b * 512)
                    pS = nps()
                    gpb = w // (R * 64)
                    for gg in range(gpb):
                        g = b * (512 // (R * 64)) + gg
                        Sc.op("pe", lambda e, g=g, gg=gg: e.matmul(pS.ap[:, gg * R * 64:(gg + 1) * R * 64], Bt.ap[:, g * 128:(g + 1) * 128], XD.ap[:, g * R:(g + 1) * R, :].rearrange("p h d -> p (h d)"), start=True, stop=True), [Bt, XD], [pS], inc=(gg == gpb - 1))
                    pSs = pSsb[b % 2]
                    Sc.op("act", lambda e: e.activation(out=pSs.ap[:, 0:w], in_=pS.ap[:, 0:w], func=AF.Copy), [pS], [pSs])
                    Sc.op("dve", lambda e: e.tensor_tensor(hfl[:, b * 512:b * 512 + w], hfl[:, b * 512:b * 512 + w], pSs.ap[:, 0:w], ALU.add), [hst, pSs], [hst])
        Sc.barrier()

        if STOP < 4:
            return nc
        with contextlib.ExitStack() as st:
            kiT = sb([128, S], BF16, "kiT", st)
            Sc.dma("sp", [(kiT.ap, kiT_d.ap)], [kiT_d], [kiT], kiT)
            kmb = sb([128, S], BF16, "kmb", st)
            Sc.dma("pool", [(kmb.ap, kmask.partition_broadcast(128))], [], [kmb], kmb)
            acc = sb([128, S], F32, "acc", st)
            mrm = sb([128, S], BF16, "mrm", st)
            maskT = sb([128, NT, 256], BF16, "maskT", st)
            qiT = sb([128, IH // 2, 128], BF16, "qiT", st)
            wit = sb([128, IH], F32, "wit", st)
            rl = [sb([128, 512], F32, "rl", st) for _ in range(3)]
            bs = sb([128, 16], F32, "bs", st)
            kTh = [sb([128, S], BF16, "kTh", st) for _ in range(2)]
            Vh = [sb([128, NT, 128], BF16, "Vh", st) for _ in range(2)]
            qTh = [sb([128, 256], BF16, "qTh", st) for _ in range(2)]
            gah = [sb([128, 256], BF16, "gah", st) for _ in range(2)]
            pb = [sb([128, 2, 256], BF16, "pb", st) for _ in range(3)]
            pm = [sb([128, 2, 256], BF16, "pm", st) for _ in range(3)]
            rs = sb([128, 256], F32, "rs", st)
            of_ = sb([128, 256], F32, "ofin", st)
            ya = [sb([128, 256], BF16, "ya", st) for _ in range(2)]
            psos = ps([128, 256], F32, "psos", st)
            psss = ps([128, 256], F32, "psss", st)
            rc = [0]
            for qg in range(NQG):
                for j in range(2):
                    tq = OT0 + 2 * qg + j
                    to = 2 * qg + j
                    NKt = (tq + 1) * 128
                    Sc.dma("sp", [(qiT.ap, qiT_d.ap[:, :, to * 128:(to + 1) * 128].rearrange("h p s -> p h s"))], [qiT_d], [qiT], qiT)
                    Sc.dma("sp", [(wit.ap, wi_d.ap[to])], [wi_d], [wit], wit)
                    for kb in range((NKt + 511) // 512):
                        nk = min(512, NKt - kb * 512)
                        for hh in range(IH):
                            pr = hh % 2
                            p = nps()
                            Sc.op("pe", lambda e: e.matmul(p.ap[:, 0:nk], qiT.ap[pr * 64:(pr + 1) * 64, hh // 2, :], kiT.ap[pr * 64:(pr + 1) * 64, kb * 512:kb * 512 + nk], start=True, stop=True), [qiT, kiT], [p])
                            rc[0] += 1
                            r_ = rl[rc[0] % 3]
                            Sc.op("act", lambda e: e.activation(out=r_.ap[:, 0:nk], in_=p.ap[:, 0:nk], func=AF.Relu), [p], [r_])
                            a_ = acc.ap[:, kb * 512:kb * 512 + nk]
                            if hh == 0:
                                Sc.op("dve", lambda e: e.tensor_scalar(a_, r_.ap[:, 0:nk], wit.ap[:, 0:1], None, op0=ALU.mult), [r_, wit], [acc])
                            else:
                                Sc.op("dve", lambda e: e.scalar_tensor_tensor(out=a_, in0=r_.ap[:, 0:nk], scalar=wit.ap[:, hh:hh + 1], in1=a_, op0=ALU.mult, op1=ALU.add), [r_, wit, acc], [acc])
                    A = acc.ap[:, 0:NKt]
                    am, lo, hi, mid, cnt, ge, d1, d2 = (bs.ap[:, i:i + 1] for i in range(8))
                    Sc.op("dve", lambda e: e.tensor_reduce(am, A, AX.X, ALU.max, apply_absolute_value=True), [acc], [bs])
                    Sc.op("dve", lambda e: e.tensor_tensor(A, A, kmb.ap[:, 0:NKt], ALU.add), [acc, kmb], [acc])
                    Sc.op("dve", lambda e: e.tensor_tensor(acc.ap[:, NKt - 128:NKt], acc.ap[:, NKt - 128:NKt], dmask_f, ALU.add), [acc, cst], [acc])
                    Sc.op("dve", lambda e: e.tensor_scalar(hi, am, 1.0, None, op0=ALU.add), [bs], [bs])
                    Sc.op("dve", lambda e: e.tensor_scalar(lo, hi, -1.0, None, op0=ALU.mult), [bs], [bs])
                    for it in range(26):
                        Sc.op("dve", lambda e: e.tensor_tensor(mid, lo, hi, ALU.add), [bs], [bs])
                        Sc.op("dve", lambda e: e.tensor_scalar(mid, mid, 0.5, None, op0=ALU.mult), [bs], [bs])
                        Sc.op("dve", lambda e: e.tensor_scalar(mrm.ap[:, 0:NKt], A, mid, None, op0=ALU.is_ge, op1=ALU.add, accum_out=cnt), [acc, bs], [mrm, bs])
                        Sc.op("dve", lambda e: e.tensor_scalar(ge, cnt, float(TOPK), None, op0=ALU.is_ge), [bs], [bs])
                        Sc.op("dve", lambda e: e.tensor_tensor(d1, mid, lo, ALU.subtract), [bs], [bs])
                        Sc.op("dve", lambda e: e.tensor_tensor(d2, hi, mid, ALU.subtract), [bs], [bs])
                        Sc.op("dve", lambda e: e.scalar_tensor_tensor(out=lo, in0=d1, scalar=ge, in1=lo, op0=ALU.mult, op1=ALU.add), [bs], [bs])
                        Sc.op("dve", lambda e: e.scalar_tensor_tensor(out=hi, in0=d2, scalar=ge, in1=mid, op0=ALU.mult, op1=ALU.add), [bs], [bs])
                    Sc.op("dve", lambda e: e.tensor_scalar(mrm.ap[:, 0:NKt], A, lo, None, op0=ALU.is_ge), [acc, bs], [mrm])
                    nkb = NKt // 128
                    for k8 in range(0, nkb, 8):
                        n8 = min(8, nkb - k8)
                        pt = npt()
                        for jj in range(n8):
                            Sc.op("pe", lambda e, jj=jj: e.transpose(pt.ap[:, jj * 128:(jj + 1) * 128], mrm.ap[:, (k8 + jj) * 128:(k8 + jj + 1) * 128], ident_b), [mrm, cbf], [pt], inc=(jj == n8 - 1))
                        cp(copy_eng(k8 // 8), maskT.ap[:, k8:k8 + n8, j * 128:(j + 1) * 128], pt.ap[:, 0:n8 * 128].rearrange("p (k n) -> p k n", n=128), [pt], [maskT])
                    if j == 0:
                        Sc.op("pool", lambda e: e.memset(maskT.ap[:, nkb:nkb + 1, 0:128], 0.0), [], [maskT])
                KBg = (OT0 + 2 * qg + 2)
                NKg = KBg * 128
                scale = 128.0 ** -0.5
                for hd in range(NH):
                    K_, V_, Q_, Ga = kTh[hd % 2], Vh[hd % 2], qTh[hd % 2], gah[hd % 2]
                    Sc.dma("sp", [(K_.ap[:, 0:NKg], kT_d.ap[hd][:, 0:NKg])], [kT_d], [K_], K_)
                    Sc.dma("sp", [(V_.ap[:, 0:KBg, :], V_d.ap[0:KBg, :, hd * 128:(hd + 1) * 128].rearrange("t p d -> p t d"))], [V_d], [V_], V_)
                    Sc.dma("sp", [(Q_.ap, qT_d.ap[hd][:, qg * 256:(qg + 1) * 256])], [qT_d], [Q_], Q_)
                    Sc.dma("sp", [(Ga.ap, gaT_d.ap[hd][:, qg * 256:(qg + 1) * 256])], [gaT_d], [Ga], Ga)
                    nk2 = KBg // 2

                    def qk(kb2):
                        pl = nps()
                        for j2 in range(2):
                            kb = kb2 * 2 + j2
                            Sc.op("pe", lambda e, j2=j2, kb=kb: e.matmul(pl.ap[:, j2 * 256:(j2 + 1) * 256], K_.ap[:, kb * 128:(kb + 1) * 128], Q_.ap, start=True, stop=True), [K_, Q_], [pl], inc=(j2 == 1))
                        return pl
                    pl_next = qk(0)
                    for kb2 in range(nk2):
                        pl = pl_next
                        if kb2 + 1 < nk2:
                            pl_next = qk(kb2 + 1)
                        rc[0] += 1
                        p_, pm_ = pb[rc[0] % 3], pm[rc[0] % 3]
                        Sc.op("act", lambda e: e.activation(out=p_.ap.rearrange("p a q -> p (a q)"), in_=pl.ap, func=AF.Exp, scale=scale), [pl], [p_])
                        Sc.op("dve", lambda e: e.tensor_tensor(pm_.ap, p_.ap, maskT.ap[:, kb2 * 2:kb2 * 2 + 2, :], ALU.mult), [p_, maskT], [pm_])
                        for j2 in range(2):
                            kb = kb2 * 2 + j2
                            first = (kb == 0)
                            last = (kb == KBg - 1)
                            Sc.op("pe", lambda e, j2=j2, kb=kb: e.matmul(psos.ap, V_.ap[:, kb, :], pm_.ap[:, j2, :], start=first, stop=last), [V_, pm_], [psos], inc=False)
                            Sc.op("pe", lambda e, j2=j2: e.matmul(psss.ap, ones_b, pm_.ap[:, j2, :], start=first, stop=last), [cbf, pm_], [psss], inc=(j2 == 1))
                    Sc.op("act", lambda e: e.activation(out=rs.ap, in_=psss.ap, func=AF.Copy), [psss], [rs])
                    Sc.op("dve", lambda e: e.reciprocal(rs.ap, rs.ap), [rs], [rs])
                    Sc.op("act", lambda e: e.activation(out=of_.ap, in_=psos.ap, func=AF.Copy), [psos], [of_])
                    Sc.op("dve", lambda e: e.tensor_tensor(of_.ap, of_.ap, rs.ap, ALU.mult), [of_, rs], [of_])
                    y_ = ya[hd % 2]
                    Sc.op("dve", lambda e: e.tensor_tensor(y_.ap, of_.ap, Ga.ap, ALU.mult), [of_, Ga], [y_])
                    Sc.dma("sp", [(yaT_d.ap[hd][:, qg * 256:(qg + 1) * 256], y_.ap)], [y_], [yaT_d], y_, disjoint=True)
        Sc.barrier()

        if STOP < 5:
            return nc
        with contextlib.ExitStack() as st:
            NKS = SW // 128
            yaT = sb([128, NH, T], BF16, "yaTs", st)
            ybT2 = sb([128, OT, NKS, 128], BF16, "ybTs", st)
            Sc.dma("sp", [(yaT.ap, yaT_d.ap.rearrange("h p s -> p h s"))], [yaT_d], [yaT], yaT)
            Sc.dma("sp", [(ybT2.ap, ybT_d.ap.rearrange("t p (k n) -> p t k n", n=128))], [ybT_d], [ybT2], ybT2)
            Wa = [sb([128, NH, 512], BF16, "Wa", st) for _ in range(2)]
            Ws = [sb([128, NKS, 512], BF16, "Ws", st) for _ in range(2)]
            gt = [sb([128, 2, 512], BF16, "gt", st) for _ in range(2)]
            m1 = [sb([128, 512], F32, "m1", st) for _ in range(2)]
            m2 = [sb([128, 512], F32, "m2", st) for _ in range(2)]
            mg = [sb([128, 512], BF16, "mgd", st) for _ in range(2)]
            mTs = [sb([128, 4, 128], BF16, "mTs", st) for _ in range(2)]
            it_ = 0
            for cb in range(D // 512):
                wa, ws = Wa[cb % 2], Ws[cb % 2]
                Sc.dma("pool", [(wa.ap, w_ba[:, cb * 512:(cb + 1) * 512].rearrange("(k p) c -> p k c", p=128))], [], [wa], wa)
                Sc.dma("pool", [(ws.ap, w_bs[:, cb * 512:(cb + 1) * 512].rearrange("(k p) c -> p k c", p=128))], [], [ws], ws)
                for to in range(OT):
                    it_ += 1
                    g_, a1, a2, mm_, mt_ = gt[it_ % 2], m1[it_ % 2], m2[it_ % 2], mg[it_ % 2], mTs[it_ % 2]
                    Sc.dma("sp", [(g_.ap, gates_d.ap[to].rearrange("p (a c) -> p a c", a=2)[:, :, cb * 512:(cb + 1) * 512])], [gates_d], [g_], g_)
                    pA, pB = nps(), nps()
                    for k in range(NH):
                        Sc.op("pe", lambda e, k=k: e.matmul(pA.ap, yaT.ap[:, k, to * 128:(to + 1) * 128], wa.ap[:, k, :], start=(k == 0), stop=(k == NH - 1)), [yaT, wa], [pA], inc=(k == NH - 1))
                    for k in range(NKS):
                        Sc.op("pe", lambda e, k=k: e.matmul(pB.ap, ybT2.ap[:, to, k, :], ws.ap[:, k, :], start=(k == 0), stop=(k == NKS - 1)), [ybT2, ws], [pB], inc=(k == NKS - 1))
                    Sc.op("act", lambda e: e.activation(out=a1.ap, in_=pA.ap, func=AF.Copy), [pA], [a1])
                    Sc.op("dve", lambda e: e.tensor_tensor(a1.ap, a1.ap, g_.ap[:, 0, :], ALU.mult), [a1, g_], [a1])
                    Sc.op("act", lambda e: e.activation(out=a2.ap, in_=pB.ap, func=AF.Copy), [pB], [a2])
                    Sc.op("dve", lambda e: e.tensor_tensor(a2.ap, a2.ap, g_.ap[:, 1, :], ALU.mult), [a2, g_], [a2])
                    Sc.op("pool", lambda e: e.tensor_tensor(mm_.ap, a1.ap, a2.ap, ALU.add), [a1, a2], [mm_])
                    pt = npt()
                    for jj in range(4):
                        Sc.op("pe", lambda e, jj=jj: e.transpose(pt.ap[:, jj * 128:(jj + 1) * 128], mm_.ap[:, jj * 128:(jj + 1) * 128], ident_b), [mm_, cbf], [pt], inc=(jj == 3))
                    cp("act", mt_.ap, pt.ap[:, 0:512].rearrange("p (k n) -> p k n", n=128), [pt], [mt_])
                    Sc.dma("sp", [(mT_d.ap[to][:, cb * 512:(cb + 1) * 512], mt_.ap.rearrange("p k n -> p (k n)"))], [mt_], [mT_d], mt_, disjoint=True)
        Sc.barrier()
        with contextlib.ExitStack() as st:
            mTa = sb([128, OT, KC, 128], BF16, "mTa", st)
            Sc.dma("sp", [(mTa.ap, mT_d.ap.rearrange("t p (k n) -> p t k n", n=128))], [mT_d], [mTa], mTa)
            Wo = [sb([128, KC, 512], BF16, "Wo", st) for _ in range(2)]
            oo = [sb([128, 512], F32, "oo", st) for _ in range(2)]
            it_ = 0
            for cb in range(D // 512):
                wo = Wo[cb % 2]
                Sc.dma("pool", [(wo.ap, w_o[:, cb * 512:(cb + 1) * 512].rearrange("(k p) c -> p k c", p=128))], [], [wo], wo)
                for to in range(OT):
                    it_ += 1
                    o_b = oo[it_ % 2]
                    p = nps()
                    for k in range(KC):
                        Sc.op("pe", lambda e, k=k: e.matmul(p.ap, mTa.ap[:, to, k, :], wo.ap[:, k, :], start=(k == 0), stop=(k == KC - 1)), [mTa, wo], [p], inc=(k == KC - 1))
                    cp("act", o_b.ap, p.ap, [p], [o_b])
                    Sc.dma("sp", [(outp_d.ap[to][:, cb * 512:(cb + 1) * 512], o_b.ap)], [o_b], [outp_d], o_b, disjoint=True)
        Sc.barrier()
        with contextlib.ExitStack() as st:
            pgb = load_row(D, D, "post_g", st)
            post_g = pgb.ap
            op_ = [sb([128, D], F32, "op", st) for _ in range(2)]
            xo = [sb([128, D], F32, "xo", st) for _ in range(2)]
            junk = sb([128, D], BF16, "junkE", st)
            ss2 = [sb([128, 4], F32, "ssE", st) for _ in range(2)]
            last = []
            for to in range(OT):
                o_b, x_b, ss = op_[to % 2], xo[to % 2], ss2[to % 2]
                Sc.dma("sp", [(o_b.ap, outp_d.ap[to])], [outp_d], [o_b], o_b)
                Sc.dma("sp", [(x_b.ap, x_loc[(OT0 + to) * 128:(OT0 + to + 1) * 128, :])], [], [x_b], x_b)
                Sc.op("act", lambda e: e.activation(out=junk.ap, in_=o_b.ap, func=AF.Square, accum_out=ss.ap[:, 0:1]), [o_b], [junk, ss])
                Sc.op("dve", lambda e: e.tensor_scalar(ss.ap[:, 1:2], ss.ap[:, 0:1], 1.0 / D, EPS, op0=ALU.mult, op1=ALU.add), [ss], [ss])
                Sc.op("act", lambda e: e.activation(out=ss.ap[:, 2:3], in_=ss.ap[:, 1:2], func=AF.Sqrt), [ss], [ss])
                Sc.op("dve", lambda e: e.reciprocal(ss.ap[:, 3:4], ss.ap[:, 2:3]), [ss], [ss])
                Sc.op("dve", lambda e: e.scalar_tensor_tensor(out=o_b.ap, in0=o_b.ap, scalar=ss.ap[:, 3:4], in1=post_g, op0=ALU.mult, op1=ALU.mult), [o_b, ss, pgb], [o_b])
                Sc.op("dve", lambda e: e.tensor_tensor(o_b.ap, o_b.ap, x_b.ap, ALU.add), [o_b, x_b], [o_b])
                last.append(Sc.dma("sp", [(out_d[to * 128:(to + 1) * 128, :], o_b.ap)], [o_b], [], o_b))
            Sc._wait("sp", last)
            Sc.barrier()
    return nc


def host_inputs(cfg, inputs):
    c = derive(cfg)
    D, S, T, NC_, SW, GN, SH, IH = c["D"], c["S"], c["T"], c["NCORE"], c["SW"], c["GN"], c["SH"], c["IH"]
    f32 = np.float32
    x = np.asarray(inputs["x"], f32)[0]
    p = np.arange(128)
    ident = np.eye(128, dtype=f32)
    pswA = np.zeros((128, 128), f32)
    pswA[(p + 64) % 128, p] = 1.0
    pswI = np.zeros((128, 128), f32)
    pswI[(p // 64) * 64 + ((p % 64) + 32) % 64, p] = 1.0
    tri = (p[:, None] <= p[None, :]).astype(f32)
    negm = np.where(p[:, None] <= p[None, :], 0.0, -30000.0).astype(f32)
    dmask = np.where(p[None, :] < ((p[:, None] // 64) + 1) * 64, 0.0, NEG).astype(f32)
    ones = np.ones((128, 128), f32)
    consts = np.ascontiguousarray(np.concatenate([ident, pswA, pswI, tri, negm, dmask, ones], 1))
    conv_w = np.asarray(inputs["conv_w"], f32)[0]
    conv_b = np.asarray(inputs["conv_b"], f32)[0]
    CH = conv_w.shape[1]
    convw = np.ascontiguousarray(conv_w.T.reshape(CH // 128, 128, 4).transpose(1, 0, 2))
    convb = np.ascontiguousarray(conv_b.reshape(CH // 128, 128).T)
    rows = np.concatenate([np.asarray(inputs[k], f32)[0] for k in
                           ("pre_norm_gain", "post_norm_gain", "ssd_norm_gain", "dt_bias", "a_log", "d_skip")])[None, :]
    rows = np.ascontiguousarray(rows)
    w_in = np.ascontiguousarray(np.asarray(inputs["w_in"], f32)[0])
    w_ba = np.ascontiguousarray(np.asarray(inputs["w_branch_attn"], f32)[0])
    w_bs = np.ascontiguousarray(np.asarray(inputs["w_branch_ssd"], f32)[0])
    w_o = np.ascontiguousarray(np.asarray(inputs["w_out"], f32)[0])
    invA = (1.0 / (10000.0 ** (np.arange(0, 128, 2, dtype=f32) / f32(128)))).astype(f32)
    invI = (1.0 / (10000.0 ** (np.arange(0, 64, 2, dtype=f32) / f32(64)))).astype(f32)
    maps = []
    for core in range(NC_):
        pad = S - (core + 1) * T
        x_loc = np.zeros((S, D), f32)
        x_loc[pad:] = x[0:(core + 1) * T]
        pos = np.maximum(np.arange(S) - pad, 0).astype(f32)
        angA = pos[:, None] * invA[None, :]
        angI = pos[:, None] * invI[None, :]
        cA, sA = np.cos(angA).astype(f32), np.sin(angA).astype(f32)
        cI, sI = np.cos(angI).astype(f32), np.sin(angI).astype(f32)
        ropeA = np.stack([np.concatenate([cA, cA], 1).T, np.concatenate([-sA, sA], 1).T]).astype(f32)
        ropeI = np.stack([np.concatenate([cI, cI, cI, cI], 1).T, np.concatenate([-sI, sI, -sI, sI], 1).T]).astype(f32)
        valid = (np.arange(S) >= pad)
        km = np.where(valid, 0.0, NEG).astype(f32)[None, :]
        vtm = np.ascontiguousarray(valid.astype(f32).reshape(S // 128, 128).T)
        maps.append(dict(x_loc=x_loc, w_in=w_in, consts=consts, ropeA=np.ascontiguousarray(ropeA),
                         ropeI=np.ascontiguousarray(ropeI), kmask=np.ascontiguousarray(km), valid_tm=vtm,
                         convw=convw, convb=convb, rows=rows, w_ba=w_ba, w_bs=w_bs, w_o=w_o))
    return maps


_NC_CACHE = {}


def run(cfg, inputs):
    key = tuple(sorted(cfg.items()))
    if key not in _NC_CACHE:
        _NC_CACHE[key] = build(cfg)
    nc = _NC_CACHE[key]
    maps = host_inputs(cfg, inputs)
    res = run_bass_kernel_spmd(nc, maps, core_ids=list(range(cfg["NCORE"])))
    out = np.concatenate([np.asarray(r["out"], np.float32) for r in res.results], 0)
    return out[None]


def kernel(**inputs):
    return run(CFG_FULL, inputs)
```

```python
import contextlib
import numpy as np
import concourse.bass as bass
import concourse.mybir as mybir
from concourse.bass_utils import run_bass_kernel_spmd

F32 = mybir.dt.float32
BF16 = mybir.dt.bfloat16
AF = mybir.ActivationFunctionType
ALU = mybir.AluOpType
AX = mybir.AxisListType

CFG_FULL = dict(D=4096, S=8192, NCORE=8, AW=2048, IH=32, SW=2048, G=8, TOPK=256)
NEG = -1.0e30


def derive(cfg):
    c = dict(cfg)
    c["T"] = c["S"] // c["NCORE"]
    c["KC"] = c["D"] // 128
    c["NH"] = c["AW"] // 128
    c["SH"] = c["SW"] // 64
    c["R"] = c["SH"] // c["G"]
    c["GN"] = c["G"] * 128
    AW, IH, SW, GN, SH, D = c["AW"], c["IH"], c["SW"], c["GN"], c["SH"], c["D"]
    offs = {}
    o = 0
    for name, w in (("q", AW), ("k", AW), ("v", AW), ("ga", AW), ("qi", IH * 64), ("ki", 64), ("wi", IH),
                    ("z", SW), ("xs", SW), ("B", GN), ("C", GN), ("dt", SH), ("mg", 2 * D)):
        offs[name] = o
        o += w
    c["offs"] = offs
    c["INW"] = o
    return c


class Buf:
    def __init__(self, ap, name):
        self.ap = ap
        self.name = name
        self.w = {}
        self.r = {}
        self.dsem = None
        self.dcnt = 0


class Sched:
    def __init__(self, nc, es):
        self.nc = nc
        self.es = es
        self.engs = {"pe": nc.tensor, "act": nc.scalar, "dve": nc.vector, "pool": nc.gpsimd, "sp": nc.sync}
        self.sem = {e: es.enter_context(nc.semaphore("sem_" + e)) for e in ("pe", "act", "dve", "pool")}
        self.cnt = {e: 0 for e in self.sem}
        self.seen = {e: {} for e in self.engs}
        self.pending_pe = False
        self.all_dma = {}

    def _wait(self, e, deps):
        best = {}
        for (sem, val, key) in deps:
            if key == "pe" and e == "pe":
                continue
            k = id(sem)
            if k not in best or best[k][1] < val:
                best[k] = (sem, val)
        for k, (sem, val) in best.items():
            if self.seen[e].get(k, 0) >= val:
                continue
            self.engs[e].wait_ge(sem, val)
            self.seen[e][k] = val

    @staticmethod
    def _merge(d, t):
        k = id(t[0])
        if k not in d or d[k][1] < t[1]:
            d[k] = t

    def _deps(self, reads, writes, disjoint):
        deps = []
        for b in reads:
            deps += list(b.w.values())
        for b in writes:
            if not disjoint:
                deps += list(b.w.values())
            deps += list(b.r.values())
        return deps

    def op(self, e, fn, reads=(), writes=(), inc=True):
        self._wait(e, self._deps(reads, writes, False))
        ins = fn(self.engs[e])
        if inc:
            self.cnt[e] += 1
            ins.then_inc(self.sem[e], 1)
            t = (self.sem[e], self.cnt[e], e)
        else:
            assert e == "pe"
            t = (self.sem[e], self.cnt[e] + 1, e)
        for b in reads:
            self._merge(b.r, t)
        for b in writes:
            self._merge(b.w, t)
        return t

    def dma(self, e, pairs, reads, writes, sb, disjoint=False):
        if sb.dsem is None:
            sb.dsem = self.es.enter_context(self.nc.semaphore("d_" + sb.name))
        self._wait(e, self._deps(reads, writes, disjoint))
        for (o, i) in pairs:
            ins = self.engs[e].dma_start(out=o, in_=i)
            sb.dcnt += 16
            ins.then_inc(sb.dsem, 16)
        t = (sb.dsem, sb.dcnt, "dma_" + sb.name)
        self.all_dma[id(sb.dsem)] = t
        for b in reads:
            self._merge(b.r, t)
        for b in writes:
            self._merge(b.w, t)
        return t

    def barrier(self):
        ts = [(self.sem[e], self.cnt[e], e) for e in self.sem if self.cnt[e] > 0] + list(self.all_dma.values())
        for e in self.engs:
            self._wait(e, [t for t in ts if not (t[2] == "pe" and e == "pe")])


def build(cfg):
    c = derive(cfg)
    D, S, T, KC, AW, NH, IH, SW, SH, G, R, GN, TOPK = (c[k] for k in
        ("D", "S", "T", "KC", "AW", "NH", "IH", "SW", "SH", "G", "R", "GN", "TOPK"))
    offs, INW = c["offs"], c["INW"]
    NT, NG = S // 128, S // 512
    OT, OG = T // 128, T // 512
    OT0, OG0 = NT - OT, NG - OG
    NQG = T // 256
    EPS = 1e-6
    nc = bass.Bass("TRN2", target_bir_lowering=False)

    def din(name, shape, dt=F32):
        return nc.dram_tensor(name, list(shape), dt, kind="ExternalInput").ap()

    def dscr(name, shape, dt):
        return Buf(nc.dram_tensor(name, list(shape), dt, kind="Internal").ap(), name)

    x_loc = din("x_loc", [S, D])
    w_in = din("w_in", [D, INW])
    consts = din("consts", [128, 7 * 128])
    ropeA = din("ropeA", [2, 128, S])
    ropeI = din("ropeI", [2, 128, S])
    kmask = din("kmask", [1, S])
    valid_tm = din("valid_tm", [128, NT])
    convw = din("convw", [128, (SW + 2 * GN) // 128, 4])
    convb = din("convb", [128, (SW + 2 * GN) // 128])
    rows = din("rows", [1, 2 * D + SW + 3 * SH])
    w_ba = din("w_ba", [AW, D])
    w_bs = din("w_bs", [SW, D])
    w_o = din("w_o", [D, D])
    out_d = nc.dram_tensor("out", [T, D], F32, kind="ExternalOutput").ap()

    hT_d = dscr("hT_d", [NG, 128, KC, 512], BF16)
    kT_d = dscr("kT_d", [NH, 128, S], BF16)
    V_d = dscr("V_d", [NT, 128, AW], BF16)
    kiT_d = dscr("kiT_d", [128, S], BF16)
    xs_d = dscr("xs_d", [NT, 128, SW], BF16)
    B_d = dscr("B_d", [NT, 128, GN], BF16)
    BT_d = dscr("BT_d", [G, 128, S], BF16)
    CT_d = dscr("CT_d", [G, 128, S], BF16)
    dt_d = dscr("dt_d", [NT, 128, SH], F32)
    qT_d = dscr("qT_d", [NH, 128, T], BF16)
    gaT_d = dscr("gaT_d", [NH, 128, T], BF16)
    qiT_d = dscr("qiT_d", [IH // 2, 128, T], BF16)
    wi_d = dscr("wi_d", [OT, 128, IH], F32)
    zs_d = dscr("zs_d", [OT, 128, SW], BF16)
    gates_d = dscr("gates_d", [OT, 128, 2 * D], BF16)
    yaT_d = dscr("yaT_d", [NH, 128, T], BF16)
    ybT_d = dscr("ybT_d", [OT, 128, SW], BF16)
    mT_d = dscr("mT_d", [OT, 128, KC * 128], BF16)
    outp_d = dscr("outp_d", [OT, 128, D], F32)

    with contextlib.ExitStack() as es:
        Sc = Sched(nc, es)
        uid = [0]

        def sb(shape, dt, name, stack=None):
            uid[0] += 1
            nm = f"{name}_{uid[0]}"
            return Buf(((stack or es).enter_context(nc.sbuf_tensor(nm, list(shape), dt)))[:], nm)

        def ps(shape, dt, name, stack=None):
            uid[0] += 1
            nm = f"{name}_{uid[0]}"
            return Buf(((stack or es).enter_context(nc.psum_tensor(nm, list(shape), dt)))[:], nm)

        cst = sb([128, 7 * 128], F32, "cst")
        Sc.dma("sp", [(cst.ap, consts)], [], [cst], cst)
        ident_f, pswA_f, pswI_f, tri_f, negm_f, dmask_f, ones_f = (cst.ap[:, i * 128:(i + 1) * 128] for i in range(7))
        cbf = sb([128, 4 * 128], BF16, "cbf")
        Sc.op("dve", lambda e: e.tensor_copy(cbf.ap[:, 0:384], cst.ap[:, 0:384]), [cst], [cbf])
        Sc.op("dve", lambda e: e.tensor_copy(cbf.ap[:, 384:512], ones_f), [cst], [cbf])
        ident_b, pswA_b, pswI_b, ones_b = (cbf.ap[:, i * 128:(i + 1) * 128] for i in range(4))
        NROW = 2 * D + SW + 3 * SH
        o_ = 2 * D + SW
        rowbc = sb([128, 3 * SH], F32, "rowbc")
        Sc.dma("sp", [(rowbc.ap, rows[:, o_:o_ + 3 * SH].partition_broadcast(128))], [], [rowbc], rowbc)
        dtb_bc = rowbc.ap[:, 0:SH]
        alog_bc = rowbc.ap[:, SH:2 * SH]
        dsk_bc = rowbc.ap[:, 2 * SH:3 * SH]

        def load_row(c0, n, name, stack):
            b = sb([128, n], F32, name, stack)
            Sc.dma("sp", [(b.ap, rows[:, c0:c0 + n].partition_broadcast(128))], [], [b], b)
            return b
        a_bc = sb([128, SH], F32, "a_bc")
        Sc.op("act", lambda e: e.activation(out=a_bc.ap, in_=alog_bc, func=AF.Exp), [rowbc], [a_bc])
        Sc.op("dve", lambda e: e.tensor_scalar(a_bc.ap, a_bc.ap, -1.0, None, op0=ALU.mult), [a_bc], [a_bc])
        vtm = sb([128, NT], F32, "vtm")
        Sc.dma("sp", [(vtm.ap, valid_tm)], [], [vtm], vtm)
        NCB = (SW + 2 * GN) // 128
        cw = sb([128, NCB, 4], F32, "cw")
        cbias = sb([128, NCB], F32, "cbias")
        Sc.dma("sp", [(cw.ap, convw)], [], [cw], cw)
        Sc.dma("sp", [(cbias.ap, convb)], [], [cbias], cbias)

        psb = [ps([128, 512], F32, f"psb{i}") for i in range(4)]
        pst = [ps([128, 1024], BF16, f"pst{i}") for i in range(2)]
        pctr = [0, 0]

        def nps():
            pctr[0] += 1
            return psb[pctr[0] % 4]

        def npt():
            pctr[1] += 1
            return pst[pctr[1] % 2]

        def copy_eng(i):
            return "act"

        def cp(e, out_ap, in_ap, reads, writes):
            if e == "act":
                return Sc.op("act", lambda q: q.activation(out=out_ap, in_=in_ap, func=AF.Copy), reads, writes)
            return Sc.op(e, lambda q: q.tensor_copy(out_ap, in_ap), reads, writes)

        with contextlib.ExitStack() as st:
            pre_gb = load_row(0, D, "pre_g", st)
            pre_g = pre_gb.ap
            xt = [sb([128, D], F32, "xt", st) for _ in range(2)]
            hb = [sb([128, D], BF16, "hb", st) for _ in range(2)]
            junk = sb([128, D], BF16, "junkA", st)
            hTt = [sb([128, KC, 512], BF16, "hTt", st) for _ in range(2)]
            ssb = [sb([128, 4], F32, "ssA", st) for _ in range(2)]
            for g in range(NG):
                hg = hTt[g % 2]
                for tt in range(4):
                    t = g * 4 + tt
                    xb, h_, ss = xt[t % 2], hb[t % 2], ssb[t % 2]
                    Sc.dma("sp", [(xb.ap, x_loc[t * 128:(t + 1) * 128, :])], [], [xb], xb)
                    Sc.op("act", lambda e: e.activation(out=junk.ap, in_=xb.ap, func=AF.Square, accum_out=ss.ap[:, 0:1]), [xb], [junk, ss])
                    Sc.op("dve", lambda e: e.tensor_scalar(ss.ap[:, 1:2], ss.ap[:, 0:1], 1.0 / D, EPS, op0=ALU.mult, op1=ALU.add), [ss], [ss])
                    Sc.op("act", lambda e: e.activation(out=ss.ap[:, 2:3], in_=ss.ap[:, 1:2], func=AF.Sqrt), [ss], [ss])
                    Sc.op("dve", lambda e: e.reciprocal(ss.ap[:, 3:4], ss.ap[:, 2:3]), [ss], [ss])
                    Sc.op("dve", lambda e: e.scalar_tensor_tensor(out=h_.ap, in0=xb.ap, scalar=ss.ap[:, 3:4], in1=pre_g, op0=ALU.mult, op1=ALU.mult), [xb, ss, pre_gb], [h_])
                    for k8 in range(0, KC, 8):
                        nk = min(8, KC - k8)
                        pt = npt()
                        for j in range(nk):
                            Sc.op("pe", lambda e, j=j: e.transpose(pt.ap[:, j * 128:(j + 1) * 128], h_.ap[:, (k8 + j) * 128:(k8 + j + 1) * 128], ident_b), [h_, cbf], [pt], inc=(j == nk - 1))
                        cp(copy_eng(k8 // 8), hg.ap[:, k8:k8 + nk, tt * 128:(tt + 1) * 128], pt.ap[:, 0:nk * 128].rearrange("p (k n) -> p k n", n=128), [pt], [hg])
                Sc.dma("sp", [(hT_d.ap[g], hg.ap)], [hg], [hT_d], hg, disjoint=True)
        Sc.barrier()

        STOP = cfg.get('STOP', 99)
        if STOP < 2:
            return nc
        with contextlib.ExitStack() as st:
            Wb = [sb([128, KC, 512], BF16, "Wb", st) for _ in range(2)]
            Hb = [sb([128, KC, 512], BF16, "Hb", st) for _ in range(2)]
            rtab = [sb([128, 4, 512], F32, "rtab", st) for _ in range(3)]
            stg = [sb([128, 4, 128], BF16, "stg", st) for _ in range(3)]
            ob = [sb([128, 512], BF16, "ob", st) for _ in range(3)]
            of = [sb([128, 512], F32, "of", st) for _ in range(4)]
            sm = [sb([128, 64], F32, "sm", st) for _ in range(2)]
            ctr = {"w": 0, "h": 0, "o": 0, "f": 0, "s": 0, "r": 0, "m": 0}

            def nxt(lst, key):
                ctr[key] += 1
                return lst[ctr[key] % len(lst)]

            pj = [0]
            DBG = cfg.get('DBG', 0)
            STOPB = cfg.get('STOPB', 999)

            def project(segs, groups, mode, cbk, need_rope=False):
                pj[0] += 1
                if pj[0] > STOPB:
                    return
                ncols = sum(n for _, n in segs)
                W = nxt(Wb, "w")
                pairs = []
                o = 0
                for (c0, n) in segs:
                    pairs.append((W.ap[:, :, o:o + n], w_in[:, c0:c0 + n].rearrange("(k p) c -> p k c", p=128)))
                    o += n
                Sc.dma("pool", pairs, [], [W], W)
                pend = []

                def load_g(g):
                    H = nxt(Hb, "h")
                    Sc.dma("sp", [(H.ap, hT_d.ap[g])], [hT_d], [H], H)
                    rt = None
                    if need_rope:
                        rt = nxt(rtab, "r")
                        Sc.dma("sp", [(rt.ap[:, 0:2, :], ropeA[:, :, g * 512:(g + 1) * 512].rearrange("a p s -> p a s")),
                                      (rt.ap[:, 2:4, :], ropeI[:, :, g * 512:(g + 1) * 512].rearrange("a p s -> p a s"))], [], [rt], rt)
                    return H, rt
                cur = load_g(groups[0])
                for gi, g in enumerate(groups):
                    H, rt = cur
                    if gi + 1 < len(groups):
                        cur = load_g(groups[gi + 1])
                    if mode == "fm":
                        for sbk in range((ncols + 127) // 128):
                            m = min(128, ncols - sbk * 128)
                            p = nps()
                            for k in range(KC):
                                Sc.op("pe", lambda e, k=k: e.matmul(p.ap[:m, :], W.ap[:, k, sbk * 128:sbk * 128 + m], H.ap[:, k, :], start=(k == 0), stop=(k == KC - 1)), [W, H], [p], inc=(k == KC - 1))
                            if pend:
                                pend.pop()()
                            pend.append(lambda sbk=sbk, m=m, g=g, gi=gi, p=p, rt=rt: cbk(sbk, m, g, gi, p, rt))
                    else:
                        for tt in range(4):
                            p = nps()
                            for k in range(KC):
                                Sc.op("pe", lambda e, k=k: e.matmul(p.ap[:, :ncols], H.ap[:, k, tt * 128:(tt + 1) * 128], W.ap[:, k, :ncols], start=(k == 0), stop=(k == KC - 1)), [W, H], [p], inc=(k == KC - 1))
                            if pend:
                                pend.pop()()
                            pend.append(lambda t_=g * 4 + tt, p=p: cbk(t_, ncols, p))
                if pend:
                    pend.pop()()

            def rope_store(p, m, rt, ti, psw_b, dst_buf, dst_ap):
                pf = nxt(of, "f")
                Sc.op("act", lambda e: e.activation(out=pf.ap[:m, :], in_=p.ap[:m, :], func=AF.Copy), [p], [pf])
                raw = nxt(ob, "o")
                Sc.op("act", lambda e: e.activation(out=raw.ap[:m, :], in_=p.ap[:m, :], func=AF.Copy), [p], [raw])
                p2 = nps()
                Sc.op("pe", lambda e: e.matmul(p2.ap[:m, :], psw_b[:m, :m], raw.ap[:m, :], start=True, stop=True), [raw, cbf], [p2])
                f3 = nxt(of, "f")
                Sc.op("act", lambda e: e.activation(out=f3.ap[:m, :], in_=p2.ap[:m, :], func=AF.Copy), [p2], [f3])
                Sc.op("dve", lambda e: e.tensor_tensor(pf.ap[:m, :], pf.ap[:m, :], rt.ap[:m, ti, :], ALU.mult), [pf, rt], [pf])
                Sc.op("dve", lambda e: e.tensor_tensor(f3.ap[:m, :], f3.ap[:m, :], rt.ap[:m, ti + 1, :], ALU.mult), [f3, rt], [f3])
                o = nxt(ob, "o")
                Sc.op("dve", lambda e: e.tensor_tensor(o.ap[:m, :], pf.ap[:m, :], f3.ap[:m, :], ALU.add), [pf, f3], [o])
                Sc.dma("sp", [(dst_ap, o.ap[:m, :])], [o], [dst_buf], o, disjoint=True)

            ALLG = list(range(NG))
            OWNG = list(range(OG0, NG))
            for cbk4 in range(AW // 512):
                def k_cb(sbk, m, g, gi, p, rt, cbk4=cbk4):
                    hd = cbk4 * 4 + sbk
                    rope_store(p, m, rt, 0, pswA_b, kT_d, kT_d.ap[hd][:, g * 512:(g + 1) * 512])
                project([(offs["k"] + cbk4 * 512, 512)], ALLG, "fm", k_cb, need_rope=True)
            for cbk4 in range(AW // 512):
                def v_cb(t, ncols, p, cbk4=cbk4):
                    o = nxt(ob, "o")
                    cp("act", o.ap, p.ap, [p], [o])
                    Sc.dma("sp", [(V_d.ap[t][:, cbk4 * 512:(cbk4 + 1) * 512], o.ap)], [o], [V_d], o, disjoint=True)
                project([(offs["v"] + cbk4 * 512, 512)], ALLG, "tm", v_cb)
            def ki_cb(sbk, m, g, gi, p, rt):
                rope_store(p, m, rt, 2, pswI_b, kiT_d, kiT_d.ap[:, g * 512:(g + 1) * 512])
            project([(offs["ki"], 64), (offs["ki"], 64)], ALLG, "fm", ki_cb, need_rope=True)

            conv_R = [[sb([128, 515], F32, "cR", st) for _ in range(2)] for _ in range(4)]

            def conv_block(col0, cblk0, nblk, groups, store):
                def cv_cb(sbk, m, g, gi, p, rt):
                    cbi = cblk0 + sbk
                    Rcur = conv_R[sbk][gi % 2]
                    Rprev = conv_R[sbk][(gi + 1) % 2]
                    if gi == 0:
                        Sc.op("pool", lambda e: e.memset(Rcur.ap[:, 0:3], 0.0), [], [Rcur])
                    else:
                        Sc.op("pool", lambda e: e.tensor_copy(Rcur.ap[:, 0:3], Rprev.ap[:, 512:515]), [Rprev], [Rcur])
                    Sc.op("act", lambda e: e.activation(out=Rcur.ap[:, 3:515], in_=p.ap, func=AF.Copy), [p], [Rcur])
                    acc = nxt(of, "f")
                    Sc.op("act", lambda e: e.activation(out=acc.ap, in_=Rcur.ap[:, 3:515], func=AF.Identity, bias=cbias.ap[:, cbi:cbi + 1], scale=cw.ap[:, cbi, 3:4]), [Rcur, cbias, cw], [acc])
                    for j in range(3):
                        Sc.op("dve", lambda e, j=j: e.scalar_tensor_tensor(out=acc.ap, in0=Rcur.ap[:, j:j + 512], scalar=cw.ap[:, cbi, j:j + 1], in1=acc.ap, op0=ALU.mult, op1=ALU.add), [Rcur, cw, acc], [acc])
                    o = nxt(ob, "o")
                    Sc.op("act", lambda e: e.activation(out=o.ap, in_=acc.ap, func=AF.Silu), [acc], [o])
                    store(sbk, g, gi, o)
                project([(col0, nblk * 128)], groups, "fm", cv_cb)

            def to_tm_store(o, g, dst_buf, dst_fn):
                pt = npt()
                for tt in range(4):
                    Sc.op("pe", lambda e, tt=tt: e.transpose(pt.ap[:, tt * 128:(tt + 1) * 128], o.ap[:, tt * 128:(tt + 1) * 128], ident_b), [o, cbf], [pt], inc=(tt == 3))
                s_ = nxt(stg, "s")
                cp("act", s_.ap, pt.ap[:, 0:512].rearrange("p (t n) -> p t n", n=128), [pt], [s_])
                Sc.dma("sp", [(dst_fn(g), s_.ap)], [s_], [dst_buf], s_, disjoint=True)

            if True:
                for cb4 in range(SW // 512):
                    def xs_store(sbk, g, gi, o, cb4=cb4):
                        c0 = cb4 * 512 + sbk * 128
                        to_tm_store(o, g, xs_d, lambda g: xs_d.ap[g * 4:(g + 1) * 4, :, c0:c0 + 128].rearrange("t p c -> p t c"))
                    conv_block(offs["xs"] + cb4 * 512, cb4 * 4, 4, ALLG, xs_store)
                for cb4 in range(GN // 512):
                    def b_store(sbk, g, gi, o, cb4=cb4):
                        gg = cb4 * 4 + sbk
                        Sc.dma("sp", [(BT_d.ap[gg][:, g * 512:(g + 1) * 512], o.ap)], [o], [BT_d], o, disjoint=True)
                        to_tm_store(o, g, B_d, lambda g: B_d.ap[g * 4:(g + 1) * 4, :, gg * 128:gg * 128 + 128].rearrange("t p c -> p t c"))
                    conv_block(offs["B"] + cb4 * 512, SW // 128 + cb4 * 4, min(4, GN // 128), ALLG, b_store)
                for cb4 in range(GN // 512):
                    def c_store(sbk, g, gi, o, cb4=cb4):
                        gg = cb4 * 4 + sbk
                        Sc.dma("sp", [(CT_d.ap[gg][:, g * 512:(g + 1) * 512], o.ap)], [o], [CT_d], o, disjoint=True)
                    conv_block(offs["C"] + cb4 * 512, (SW + GN) // 128 + cb4 * 4, min(4, GN // 128), [OG0 - 1] + OWNG, c_store)

            def dt_cb(t, ncols, p):
                s_ = nxt(sm, "m")
                Sc.op("act", lambda e: e.activation(out=s_.ap[:, 0:SH], in_=p.ap[:, 0:SH], func=AF.Copy), [p], [s_])
                Sc.op("dve", lambda e: e.tensor_tensor(s_.ap[:, 0:SH], s_.ap[:, 0:SH], dtb_bc, ALU.add), [s_, rowbc], [s_])
                Sc.op("act", lambda e: e.activation(out=s_.ap[:, 0:SH], in_=s_.ap[:, 0:SH], func=AF.Exp), [s_], [s_])
                Sc.op("act", lambda e: e.activation(out=s_.ap[:, 0:SH], in_=s_.ap[:, 0:SH], func=AF.Ln, bias=1.0), [s_], [s_])
                Sc.op("dve", lambda e: e.tensor_scalar(s_.ap[:, 0:SH], s_.ap[:, 0:SH], vtm.ap[:, t:t + 1], None, op0=ALU.mult), [s_, vtm], [s_])
                Sc.dma("sp", [(dt_d.ap[t], s_.ap[:, 0:SH])], [s_], [dt_d], s_, disjoint=True)
            project([(offs["dt"], SH)], ALLG, "tm", dt_cb)

            for cbk4 in range(AW // 512):
                def q_cb(sbk, m, g, gi, p, rt, cbk4=cbk4):
                    hd = cbk4 * 4 + sbk
                    rope_store(p, m, rt, 0, pswA_b, qT_d, qT_d.ap[hd][:, gi * 512:(gi + 1) * 512])
                project([(offs["q"] + cbk4 * 512, 512)], OWNG, "fm", q_cb, need_rope=True)
            for cbk4 in range(AW // 512):
                def ga_cb(sbk, m, g, gi, p, rt, cbk4=cbk4):
                    hd = cbk4 * 4 + sbk
                    o = nxt(ob, "o")
                    Sc.op("act", lambda e: e.activation(out=o.ap, in_=p.ap, func=AF.Silu), [p], [o])
                    Sc.dma("sp", [(gaT_d.ap[hd][:, gi * 512:(gi + 1) * 512], o.ap)], [o], [gaT_d], o, disjoint=True)
                project([(offs["ga"] + cbk4 * 512, 512)], OWNG, "fm", ga_cb)
            for cbk4 in range(IH * 64 // 512):
                def qi_cb(sbk, m, g, gi, p, rt, cbk4=cbk4):
                    hp = cbk4 * 4 + sbk
                    rope_store(p, m, rt, 2, pswI_b, qiT_d, qiT_d.ap[hp][:, gi * 512:(gi + 1) * 512])
                project([(offs["qi"] + cbk4 * 512, min(512, IH * 64))], OWNG, "fm", qi_cb, need_rope=True)
            wscale = float(IH ** -0.5 * 64 ** -0.5)

            def wi_cb(t, ncols, p):
                s_ = nxt(sm, "m")
                Sc.op("act", lambda e: e.activation(out=s_.ap[:, 0:IH], in_=p.ap[:, 0:IH], func=AF.Copy, scale=wscale), [p], [s_])
                Sc.dma("sp", [(wi_d.ap[t - OT0], s_.ap[:, 0:IH])], [s_], [wi_d], s_, disjoint=True)
            project([(offs["wi"], IH)], OWNG, "tm", wi_cb)
            for cb4 in range(SW // 512):
                def z_cb(t, ncols, p, cb4=cb4):
                    o = nxt(ob, "o")
                    Sc.op("act", lambda e: e.activation(out=o.ap, in_=p.ap, func=AF.Silu), [p], [o])
                    Sc.dma("sp", [(zs_d.ap[t - OT0][:, cb4 * 512:(cb4 + 1) * 512], o.ap)], [o], [zs_d], o, disjoint=True)
                project([(offs["z"] + cb4 * 512, 512)], OWNG, "tm", z_cb)
            for cb4 in range(2 * D // 512):
                def mg_cb(t, ncols, p, cb4=cb4):
                    o = nxt(ob, "o")
                    Sc.op("act", lambda e: e.activation(out=o.ap, in_=p.ap, func=AF.Sigmoid), [p], [o])
                    Sc.dma("sp", [(gates_d.ap[t - OT0][:, cb4 * 512:(cb4 + 1) * 512], o.ap)], [o], [gates_d], o, disjoint=True)
                project([(offs["mg"] + cb4 * 512, 512)], OWNG, "tm", mg_cb)
        Sc.barrier()

        if STOP < 3:
            return nc
        with contextlib.ExitStack() as st:
            ngb = load_row(2 * D, SW, "ngain", st)
            ngain = ngb.ap
            hst = sb([128, SH, 64], F32, "hst", st)
            Sc.op("pool", lambda e: e.memset(hst.ap, 0.0), [], [hst])
            xsb = [sb([128, SH, 64], BF16, "xsb", st) for _ in range(2)]
            Bb = [sb([128, GN], BF16, "Bb", st) for _ in range(2)]
            dtb = [sb([128, SH], F32, "dtb", st) for _ in range(2)]
            sc_ = [sb([128, 8, SH], F32, "sc", st) for _ in range(2)]
            xdtd = [sb([128, SH, 64], BF16, "xdtd", st) for _ in range(2)]
            xdt = sb([128, SH, 64], BF16, "xdt", st)
            hbf = sb([128, SH, 64], BF16, "hbf", st)
            Rm = sb([128, SH, 128], F32, "Rm", st)
            E4 = [sb([128, 4, 128], F32, "E4", st) for _ in range(2)]
            dec = [sb([128, 128], F32, "dec", st) for _ in range(2)]
            ea = [sb([128, 128], F32, "ea", st) for _ in range(2)]
            MT = [sb([128, 128], BF16, "MT", st) for _ in range(2)]
            CsT = [sb([128, 128], BF16, "CsT", st) for _ in range(2)]
            cbT = [sb([128, 128], F32, "cbT", st) for _ in range(G)]
            BTt = sb([128, G, 128], BF16, "BTt", st)
            CTt = sb([128, G, 128], BF16, "CTt", st)
            ysb = sb([128, SH, 64], F32, "ysb", st)
            tmpy = sb([128, 4, 64], F32, "tmpy", st)
            zsb = sb([128, SW], BF16, "zsb", st)
            ybn = sb([128, SW], BF16, "ybn", st)
            junkc = sb([128, SW // G], BF16, "junkc", st)
            ssq = sb([128, 3 * G], F32, "ssq", st)
            ybT = sb([128, SW // 128, 128], BF16, "ybTt", st)
            pSsb = [sb([128, 512], F32, "pSs", st) for _ in range(2)]
            def load_t(t):
                X, Bt, dtt = xsb[t % 2], Bb[t % 2], dtb[t % 2]
                Sc.dma("sp", [(X.ap.rearrange("p h d -> p (h d)"), xs_d.ap[t])], [xs_d], [X], X)
                Sc.dma("sp", [(Bt.ap, B_d.ap[t])], [B_d], [Bt], Bt)
                Sc.dma("sp", [(dtt.ap, dt_d.ap[t])], [dt_d], [dtt], dtt)
            load_t(0)
            for t in range(NT):
                X, Bt, dtt, s_ = xsb[t % 2], Bb[t % 2], dtb[t % 2], sc_[t % 2]
                if t + 1 < NT:
                    load_t(t + 1)
                dta, acum, dte, cd, wgt, nac, aend = (s_.ap[:, i, :] for i in range(7))
                Sc.op("dve", lambda e: e.tensor_tensor(dta, dtt.ap, a_bc.ap, ALU.mult), [dtt, a_bc], [s_])
                pa, pe_ = nps(), nps()
                Sc.op("pe", lambda e: e.matmul(pa.ap[:, 0:SH], tri_f, dta, start=True, stop=True), [cst, s_], [pa])
                Sc.op("pe", lambda e: e.matmul(pe_.ap[:, 0:SH], ones_f, dta, start=True, stop=True), [cst, s_], [pe_])
                Sc.op("act", lambda e: e.activation(out=acum, in_=pa.ap[:, 0:SH], func=AF.Copy), [pa], [s_])
                Sc.op("act", lambda e: e.activation(out=aend, in_=pe_.ap[:, 0:SH], func=AF.Copy), [pe_], [s_])
                Sc.op("dve", lambda e: e.tensor_tensor(dte, aend, acum, ALU.subtract), [s_], [s_])
                Sc.op("act", lambda e: e.activation(out=dte, in_=dte, func=AF.Exp), [s_], [s_])
                Sc.op("act", lambda e: e.activation(out=cd, in_=pe_.ap[:, 0:SH], func=AF.Exp), [pe_], [s_])
                Sc.op("dve", lambda e: e.tensor_tensor(wgt, dtt.ap, dte, ALU.mult), [dtt, s_], [s_])
                XD = xdtd[t % 2]
                Sc.op("pool", lambda e: e.tensor_tensor(XD.ap, X.ap, wgt.unsqueeze(2).broadcast_to([128, SH, 64]), ALU.mult), [X, s_], [XD])
                if t >= OT0:
                    to = t - OT0
                    Sc.op("dve", lambda e: e.tensor_scalar(nac, acum, -1.0, None, op0=ALU.mult), [s_], [s_])
                    Sc.op("pool", lambda e: e.tensor_tensor(xdt.ap, X.ap, dtt.ap.unsqueeze(2).broadcast_to([128, SH, 64]), ALU.mult), [X, dtt], [xdt])
                    Sc.op("act", lambda e: e.activation(out=hbf.ap, in_=hst.ap, func=AF.Copy), [hst], [hbf])
                    Sc.dma("sp", [(BTt.ap, BT_d.ap[:, :, t * 128:(t + 1) * 128].rearrange("g p s -> p g s"))], [BT_d], [BTt], BTt)
                    Sc.dma("sp", [(CTt.ap, CT_d.ap[:, :, t * 128:(t + 1) * 128].rearrange("g p s -> p g s"))], [CT_d], [CTt], CTt)
                    Sc.dma("sp", [(zsb.ap, zs_d.ap[to])], [zs_d], [zsb], zsb)
                    for g in range(G):
                        pc = nps()
                        Sc.op("pe", lambda e, g=g: e.matmul(pc.ap[:, 0:128], BTt.ap[:, g, :], CTt.ap[:, g, :], start=True, stop=True), [BTt, CTt], [pc])
                        cp("act", cbT[g].ap, pc.ap[:, 0:128], [pc], [cbT[g]])
                    Sc.op("pool", lambda e: e.tensor_tensor(Rm.ap, tri_f.unsqueeze(1).broadcast_to([128, SH, 128]), dta.unsqueeze(2).broadcast_to([128, SH, 128]), ALU.mult), [cst, s_], [Rm])
                    for hg in range(SH // 4):
                        pbc = nps()
                        Sc.op("pe", lambda e: e.matmul(pbc.ap, ones_f, Rm.ap[:, hg * 4:(hg + 1) * 4, :].rearrange("p h l -> p (h l)"), start=True, stop=True), [cst, Rm], [pbc])
                        e4 = E4[hg % 2]
                        Sc.op("act", lambda e: e.activation(out=e4.ap.rearrange("p h l -> p (h l)"), in_=pbc.ap, func=AF.Copy), [pbc], [e4])
                        Sc.op("dve", lambda e: e.tensor_tensor(e4.ap, e4.ap, negm_f.unsqueeze(1).broadcast_to([128, 4, 128]), ALU.add), [e4, cst], [e4])
                        py = nps()
                        for j in range(4):
                            hh = hg * 4 + j
                            g = hh // R
                            d_, ea_, mt_, cs_ = dec[j % 2], ea[j % 2], MT[j % 2], CsT[j % 2]
                            Sc.op("act", lambda e: e.activation(out=d_.ap, in_=e4.ap[:, j, :], func=AF.Exp, bias=nac[:, hh:hh + 1]), [e4, s_], [d_])
                            Sc.op("dve", lambda e: e.tensor_tensor(mt_.ap, cbT[g].ap, d_.ap, ALU.mult), [cbT[g], d_], [mt_])
                            Sc.op("act", lambda e: e.activation(out=ea_.ap, in_=pbc.ap[:, j * 128:(j + 1) * 128], func=AF.Exp), [pbc], [ea_])
                            Sc.op("dve", lambda e: e.tensor_tensor(cs_.ap, CTt.ap[:, g, :], ea_.ap, ALU.mult), [CTt, ea_], [cs_])
                            Sc.op("pe", lambda e: e.matmul(py.ap[:, j * 64:(j + 1) * 64], mt_.ap, xdt.ap[:, hh, :], start=True, stop=False), [mt_, xdt], [py], inc=False)
                            Sc.op("pe", lambda e: e.matmul(py.ap[:, j * 64:(j + 1) * 64], cs_.ap, hbf.ap[:, hh, :], start=False, stop=True), [cs_, hbf], [py])
                        Sc.op("dve", lambda e: e.tensor_tensor(tmpy.ap, X.ap[:, hg * 4:(hg + 1) * 4, :], dsk_bc[:, hg * 4:(hg + 1) * 4].unsqueeze(2).broadcast_to([128, 4, 64]), ALU.mult), [X, rowbc], [tmpy])
                        Sc.op("act", lambda e: e.activation(out=ysb.ap[:, hg * 4:(hg + 1) * 4, :].rearrange("p h d -> p (h d)"), in_=py.ap[:, 0:256], func=AF.Copy), [py], [ysb])
                        Sc.op("dve", lambda e: e.tensor_tensor(ysb.ap[:, hg * 4:(hg + 1) * 4, :], ysb.ap[:, hg * 4:(hg + 1) * 4, :], tmpy.ap, ALU.add), [ysb, tmpy], [ysb])
                    yflat = ysb.ap.rearrange("p h d -> p (h d)")
                    Sc.op("dve", lambda e: e.tensor_tensor(yflat, yflat, zsb.ap, ALU.mult), [ysb, zsb], [ysb])
                    GW = SW // G
                    for g in range(G):
                        Sc.op("act", lambda e, g=g: e.activation(out=junkc.ap, in_=yflat[:, g * GW:(g + 1) * GW], func=AF.Square, accum_out=ssq.ap[:, g:g + 1]), [ysb], [junkc, ssq])
                    Sc.op("dve", lambda e: e.tensor_scalar(ssq.ap[:, G:2 * G], ssq.ap[:, 0:G], 1.0 / GW, EPS, op0=ALU.mult, op1=ALU.add), [ssq], [ssq])
                    Sc.op("act", lambda e: e.activation(out=ssq.ap[:, G:2 * G], in_=ssq.ap[:, G:2 * G], func=AF.Sqrt), [ssq], [ssq])
                    Sc.op("dve", lambda e: e.reciprocal(ssq.ap[:, 2 * G:3 * G], ssq.ap[:, G:2 * G]), [ssq], [ssq])
                    for g in range(G):
                        Sc.op("dve", lambda e, g=g: e.scalar_tensor_tensor(out=ybn.ap[:, g * GW:(g + 1) * GW], in0=yflat[:, g * GW:(g + 1) * GW], scalar=ssq.ap[:, 2 * G + g:2 * G + g + 1], in1=ngain[:, g * GW:(g + 1) * GW], op0=ALU.mult, op1=ALU.mult), [ysb, ssq, ngb], [ybn])
                    for k8 in range(0, SW // 128, 8):
                        nk = min(8, SW // 128 - k8)
                        pt = npt()
                        for j in range(nk):
                            Sc.op("pe", lambda e, j=j: e.transpose(pt.ap[:, j * 128:(j + 1) * 128], ybn.ap[:, (k8 + j) * 128:(k8 + j + 1) * 128], ident_b), [ybn, cbf], [pt], inc=(j == nk - 1))
                        cp("act", ybT.ap[:, k8:k8 + nk, :], pt.ap[:, 0:nk * 128].rearrange("p (k n) -> p k n", n=128), [pt], [ybT])
                    Sc.dma("sp", [(ybT_d.ap[to], ybT.ap.rearrange("p k n -> p (k n)"))], [ybT], [ybT_d], ybT, disjoint=True)
                hfl = hst.ap.rearrange("p h d -> p (h d)")
                Sc.op("dve", lambda e: e.tensor_tensor(hst.ap, hst.ap, cd.unsqueeze(2).broadcast_to([128, SH, 64]), ALU.mult), [hst, s_], [hst])
                for b in range((SW + 511) // 512):
                    w = min(512, SW - b * 512)
                    pS = nps()
                    gpb = w // (R * 64)
                    for gg in range(gpb):
                        g = b * (512 // (R * 64)) + gg
                        Sc.op("pe", lambda e, g=g, gg=gg: e.matmul(pS.ap[:, gg * R * 64:(gg + 1) * R * 64], Bt.ap[:, g * 128:(g + 1) * 128], XD.ap[:, g * R:(g + 1) * R, :].rearrange("p h d -> p (h d)"), start=True, stop=True), [Bt, XD], [pS], inc=(gg == gpb - 1))
                    pSs = pSsb[b % 2]
                    Sc.op("act", lambda e: e.activation(out=pSs.ap[:, 0:w], in_=pS.ap[:, 0:w], func=AF.Copy), [pS], [pSs])
                    Sc.op("dve", lambda e: e.tensor_tensor(hfl[:, b * 512:b * 512 + w], hfl[:, b * 512:b * 512 + w], pSs.ap[:, 0:w], ALU.add), [hst, pSs], [hst])
        Sc.barrier()

        if STOP < 4:
            return nc
        with contextlib.ExitStack() as st:
            kiT = sb([128, S], BF16, "kiT", st)
            Sc.dma("sp", [(kiT.ap, kiT_d.ap)], [kiT_d], [kiT], kiT)
            kmb = sb([128, S], BF16, "kmb", st)
            Sc.dma("pool", [(kmb.ap, kmask.partition_broadcast(128))], [], [kmb], kmb)
            acc = sb([128, S], F32, "acc", st)
            mrm = sb([128, S], BF16, "mrm", st)
            maskT = sb([128, NT, 256], BF16, "maskT", st)
            qiT = sb([128, IH // 2, 128], BF16, "qiT", st)
            wit = sb([128, IH], F32, "wit", st)
            rl = [sb([128, 512], F32, "rl", st) for _ in range(3)]
            bs = sb([128, 16], F32, "bs", st)
            kTh = [sb([128, S], BF16, "kTh", st) for _ in range(2)]
            Vh = [sb([128, NT, 128], BF16, "Vh", st) for _ in range(2)]
            qTh = [sb([128, 256], BF16, "qTh", st) for _ in range(2)]
            gah = [sb([128, 256], BF16, "gah", st) for _ in range(2)]
            pb = [sb([128, 2, 256], BF16, "pb", st) for _ in range(3)]
            pm = [sb([128, 2, 256], BF16, "pm", st) for _ in range(3)]
            rs = sb([128, 256], F32, "rs", st)
            of_ = sb([128, 256], F32, "ofin", st)
            ya = [sb([128, 256], BF16, "ya", st) for _ in range(2)]
            psos = ps([128, 256], F32, "psos", st)
            psss = ps([128, 256], F32, "psss", st)
            rc = [0]
            for qg in range(NQG):
                for j in range(2):
                    tq = OT0 + 2 * qg + j
                    to = 2 * qg + j
                    NKt = (tq + 1) * 128
                    Sc.dma("sp", [(qiT.ap, qiT_d.ap[:, :, to * 128:(to + 1) * 128].rearrange("h p s -> p h s"))], [qiT_d], [qiT], qiT)
                    Sc.dma("sp", [(wit.ap, wi_d.ap[to])], [wi_d], [wit], wit)
                    for kb in range((NKt + 511) // 512):
                        nk = min(512, NKt - kb * 512)
                        for hh in range(IH):
                            pr = hh % 2
                            p = nps()
                            Sc.op("pe", lambda e: e.matmul(p.ap[:, 0:nk], qiT.ap[pr * 64:(pr + 1) * 64, hh // 2, :], kiT.ap[pr * 64:(pr + 1) * 64, kb * 512:kb * 512 + nk], start=True, stop=True), [qiT, kiT], [p])
                            rc[0] += 1
                            r_ = rl[rc[0] % 3]
                            Sc.op("act", lambda e: e.activation(out=r_.ap[:, 0:nk], in_=p.ap[:, 0:nk], func=AF.Relu), [p], [r_])
                            a_ = acc.ap[:, kb * 512:kb * 512 + nk]
                            if hh == 0:
                                Sc.op("dve", lambda e: e.tensor_scalar(a_, r_.ap[:, 0:nk], wit.ap[:, 0:1], None, op0=ALU.mult), [r_, wit], [acc])
                            else:
                                Sc.op("dve", lambda e: e.scalar_tensor_tensor(out=a_, in0=r_.ap[:, 0:nk], scalar=wit.ap[:, hh:hh + 1], in1=a_, op0=ALU.mult, op1=ALU.add), [r_, wit, acc], [acc])
                    A = acc.ap[:, 0:NKt]
                    am, lo, hi, mid, cnt, ge, d1, d2 = (bs.ap[:, i:i + 1] for i in range(8))
                    Sc.op("dve", lambda e: e.tensor_reduce(am, A, AX.X, ALU.max, apply_absolute_value=True), [acc], [bs])
                    Sc.op("dve", lambda e: e.tensor_tensor(A, A, kmb.ap[:, 0:NKt], ALU.add), [acc, kmb], [acc])
                    Sc.op("dve", lambda e: e.tensor_tensor(acc.ap[:, NKt - 128:NKt], acc.ap[:, NKt - 128:NKt], dmask_f, ALU.add), [acc, cst], [acc])
                    Sc.op("dve", lambda e: e.tensor_scalar(hi, am, 1.0, None, op0=ALU.add), [bs], [bs])
                    Sc.op("dve", lambda e: e.tensor_scalar(lo, hi, -1.0, None, op0=ALU.mult), [bs], [bs])
                    for it in range(22):
                        Sc.op("dve", lambda e: e.tensor_tensor(mid, lo, hi, ALU.add), [bs], [bs])
                        Sc.op("dve", lambda e: e.tensor_scalar(mid, mid, 0.5, None, op0=ALU.mult), [bs], [bs])
                        Sc.op("dve", lambda e: e.tensor_scalar(mrm.ap[:, 0:NKt], A, mid, None, op0=ALU.is_ge, op1=ALU.add, accum_out=cnt), [acc, bs], [mrm, bs])
                        Sc.op("dve", lambda e: e.tensor_scalar(ge, cnt, float(TOPK), None, op0=ALU.is_ge), [bs], [bs])
                        Sc.op("dve", lambda e: e.tensor_tensor(d1, mid, lo, ALU.subtract), [bs], [bs])
                        Sc.op("dve", lambda e: e.tensor_tensor(d2, hi, mid, ALU.subtract), [bs], [bs])
                        Sc.op("dve", lambda e: e.scalar_tensor_tensor(out=lo, in0=d1, scalar=ge, in1=lo, op0=ALU.mult, op1=ALU.add), [bs], [bs])
                        Sc.op("dve", lambda e: e.scalar_tensor_tensor(out=hi, in0=d2, scalar=ge, in1=mid, op0=ALU.mult, op1=ALU.add), [bs], [bs])
                    Sc.op("dve", lambda e: e.tensor_scalar(mrm.ap[:, 0:NKt], A, lo, None, op0=ALU.is_ge), [acc, bs], [mrm])
                    nkb = NKt // 128
                    for k8 in range(0, nkb, 8):
                        n8 = min(8, nkb - k8)
                        pt = npt()
                        for jj in range(n8):
                            Sc.op("pe", lambda e, jj=jj: e.transpose(pt.ap[:, jj * 128:(jj + 1) * 128], mrm.ap[:, (k8 + jj) * 128:(k8 + jj + 1) * 128], ident_b), [mrm, cbf], [pt], inc=(jj == n8 - 1))
                        cp(copy_eng(k8 // 8), maskT.ap[:, k8:k8 + n8, j * 128:(j + 1) * 128], pt.ap[:, 0:n8 * 128].rearrange("p (k n) -> p k n", n=128), [pt], [maskT])
                    if j == 0:
                        Sc.op("pool", lambda e: e.memset(maskT.ap[:, nkb:nkb + 1, 0:128], 0.0), [], [maskT])
                KBg = (OT0 + 2 * qg + 2)
                NKg = KBg * 128
                scale = 128.0 ** -0.5
                def load_hd(hd):
                    K_, V_, Q_, Ga = kTh[hd % 2], Vh[hd % 2], qTh[hd % 2], gah[hd % 2]
                    Sc.dma("sp", [(K_.ap[:, 0:NKg], kT_d.ap[hd][:, 0:NKg])], [kT_d], [K_], K_)
                    Sc.dma("sp", [(V_.ap[:, 0:KBg, :], V_d.ap[0:KBg, :, hd * 128:(hd + 1) * 128].rearrange("t p d -> p t d"))], [V_d], [V_], V_)
                    Sc.dma("sp", [(Q_.ap, qT_d.ap[hd][:, qg * 256:(qg + 1) * 256])], [qT_d], [Q_], Q_)
                    Sc.dma("sp", [(Ga.ap, gaT_d.ap[hd][:, qg * 256:(qg + 1) * 256])], [gaT_d], [Ga], Ga)
                load_hd(0)
                for hd in range(NH):
                    K_, V_, Q_, Ga = kTh[hd % 2], Vh[hd % 2], qTh[hd % 2], gah[hd % 2]
                    if hd + 1 < NH:
                        load_hd(hd + 1)
                    nk2 = KBg // 2

                    def qk(kb2):
                        pl = nps()
                        for j2 in range(2):
                            kb = kb2 * 2 + j2
                            Sc.op("pe", lambda e, j2=j2, kb=kb: e.matmul(pl.ap[:, j2 * 256:(j2 + 1) * 256], K_.ap[:, kb * 128:(kb + 1) * 128], Q_.ap, start=True, stop=True), [K_, Q_], [pl], inc=(j2 == 1))
                        return pl
                    LOOK = 2
                    pls = [qk(i) for i in range(min(LOOK, nk2))]
                    for kb2 in range(nk2):
                        pl = pls.pop(0)
                        if kb2 + LOOK < nk2:
                            pls.append(qk(kb2 + LOOK))
                        rc[0] += 1
                        p_, pm_ = pb[rc[0] % 3], pm[rc[0] % 3]
                        Sc.op("act", lambda e: e.activation(out=p_.ap.rearrange("p a q -> p (a q)"), in_=pl.ap, func=AF.Exp, scale=scale), [pl], [p_])
                        Sc.op("dve", lambda e: e.tensor_tensor(pm_.ap, p_.ap, maskT.ap[:, kb2 * 2:kb2 * 2 + 2, :], ALU.mult), [p_, maskT], [pm_])
                        for j2 in range(2):
                            kb = kb2 * 2 + j2
                            first = (kb == 0)
                            last = (kb == KBg - 1)
                            Sc.op("pe", lambda e, j2=j2, kb=kb: e.matmul(psos.ap, V_.ap[:, kb, :], pm_.ap[:, j2, :], start=first, stop=last), [V_, pm_], [psos], inc=False)
                            Sc.op("pe", lambda e, j2=j2: e.matmul(psss.ap, ones_b, pm_.ap[:, j2, :], start=first, stop=last), [cbf, pm_], [psss], inc=(j2 == 1))
                    Sc.op("act", lambda e: e.activation(out=rs.ap, in_=psss.ap, func=AF.Copy), [psss], [rs])
                    Sc.op("dve", lambda e: e.reciprocal(rs.ap, rs.ap), [rs], [rs])
                    Sc.op("act", lambda e: e.activation(out=of_.ap, in_=psos.ap, func=AF.Copy), [psos], [of_])
                    Sc.op("dve", lambda e: e.tensor_tensor(of_.ap, of_.ap, rs.ap, ALU.mult), [of_, rs], [of_])
                    y_ = ya[hd % 2]
                    Sc.op("dve", lambda e: e.tensor_tensor(y_.ap, of_.ap, Ga.ap, ALU.mult), [of_, Ga], [y_])
                    Sc.dma("sp", [(yaT_d.ap[hd][:, qg * 256:(qg + 1) * 256], y_.ap)], [y_], [yaT_d], y_, disjoint=True)
        Sc.barrier()

        if STOP < 5:
            return nc
        with contextlib.ExitStack() as st:
            NKS = SW // 128
            yaT = sb([128, NH, T], BF16, "yaTs", st)
            ybT2 = sb([128, OT, NKS, 128], BF16, "ybTs", st)
            Sc.dma("sp", [(yaT.ap, yaT_d.ap.rearrange("h p s -> p h s"))], [yaT_d], [yaT], yaT)
            Sc.dma("sp", [(ybT2.ap, ybT_d.ap.rearrange("t p (k n) -> p t k n", n=128))], [ybT_d], [ybT2], ybT2)
            Wa = [sb([128, NH, 512], BF16, "Wa", st) for _ in range(2)]
            Ws = [sb([128, NKS, 512], BF16, "Ws", st) for _ in range(2)]
            gt = [sb([128, 2, 512], BF16, "gt", st) for _ in range(2)]
            m1 = [sb([128, 512], F32, "m1", st) for _ in range(2)]
            m2 = [sb([128, 512], F32, "m2", st) for _ in range(2)]
            mg = [sb([128, 512], BF16, "mgd", st) for _ in range(2)]
            mTs = [sb([128, 4, 128], BF16, "mTs", st) for _ in range(2)]
            it_ = 0
            for cb in range(D // 512):
                wa, ws = Wa[cb % 2], Ws[cb % 2]
                Sc.dma("pool", [(wa.ap, w_ba[:, cb * 512:(cb + 1) * 512].rearrange("(k p) c -> p k c", p=128))], [], [wa], wa)
                Sc.dma("pool", [(ws.ap, w_bs[:, cb * 512:(cb + 1) * 512].rearrange("(k p) c -> p k c", p=128))], [], [ws], ws)
                for to in range(OT):
                    it_ += 1
                    g_, a1, a2, mm_, mt_ = gt[it_ % 2], m1[it_ % 2], m2[it_ % 2], mg[it_ % 2], mTs[it_ % 2]
                    Sc.dma("sp", [(g_.ap, gates_d.ap[to].rearrange("p (a c) -> p a c", a=2)[:, :, cb * 512:(cb + 1) * 512])], [gates_d], [g_], g_)
                    pA, pB = nps(), nps()
                    for k in range(NH):
                        Sc.op("pe", lambda e, k=k: e.matmul(pA.ap, yaT.ap[:, k, to * 128:(to + 1) * 128], wa.ap[:, k, :], start=(k == 0), stop=(k == NH - 1)), [yaT, wa], [pA], inc=(k == NH - 1))
                    for k in range(NKS):
                        Sc.op("pe", lambda e, k=k: e.matmul(pB.ap, ybT2.ap[:, to, k, :], ws.ap[:, k, :], start=(k == 0), stop=(k == NKS - 1)), [ybT2, ws], [pB], inc=(k == NKS - 1))
                    Sc.op("act", lambda e: e.activation(out=a1.ap, in_=pA.ap, func=AF.Copy), [pA], [a1])
                    Sc.op("dve", lambda e: e.tensor_tensor(a1.ap, a1.ap, g_.ap[:, 0, :], ALU.mult), [a1, g_], [a1])
                    Sc.op("act", lambda e: e.activation(out=a2.ap, in_=pB.ap, func=AF.Copy), [pB], [a2])
                    Sc.op("dve", lambda e: e.tensor_tensor(a2.ap, a2.ap, g_.ap[:, 1, :], ALU.mult), [a2, g_], [a2])
                    Sc.op("pool", lambda e: e.tensor_tensor(mm_.ap, a1.ap, a2.ap, ALU.add), [a1, a2], [mm_])
                    pt = npt()
                    for jj in range(4):
                        Sc.op("pe", lambda e, jj=jj: e.transpose(pt.ap[:, jj * 128:(jj + 1) * 128], mm_.ap[:, jj * 128:(jj + 1) * 128], ident_b), [mm_, cbf], [pt], inc=(jj == 3))
                    cp("act", mt_.ap, pt.ap[:, 0:512].rearrange("p (k n) -> p k n", n=128), [pt], [mt_])
                    Sc.dma("sp", [(mT_d.ap[to][:, cb * 512:(cb + 1) * 512], mt_.ap.rearrange("p k n -> p (k n)"))], [mt_], [mT_d], mt_, disjoint=True)
        Sc.barrier()
        with contextlib.ExitStack() as st:
            mTa = sb([128, OT, KC, 128], BF16, "mTa", st)
            Sc.dma("sp", [(mTa.ap, mT_d.ap.rearrange("t p (k n) -> p t k n", n=128))], [mT_d], [mTa], mTa)
            Wo = [sb([128, KC, 512], BF16, "Wo", st) for _ in range(2)]
            oo = [sb([128, 512], F32, "oo", st) for _ in range(2)]
            it_ = 0
            for cb in range(D // 512):
                wo = Wo[cb % 2]
                Sc.dma("pool", [(wo.ap, w_o[:, cb * 512:(cb + 1) * 512].rearrange("(k p) c -> p k c", p=128))], [], [wo], wo)
                for to in range(OT):
                    it_ += 1
                    o_b = oo[it_ % 2]
                    p = nps()
                    for k in range(KC):
                        Sc.op("pe", lambda e, k=k: e.matmul(p.ap, mTa.ap[:, to, k, :], wo.ap[:, k, :], start=(k == 0), stop=(k == KC - 1)), [mTa, wo], [p], inc=(k == KC - 1))
                    cp("act", o_b.ap, p.ap, [p], [o_b])
                    Sc.dma("sp", [(outp_d.ap[to][:, cb * 512:(cb + 1) * 512], o_b.ap)], [o_b], [outp_d], o_b, disjoint=True)
        Sc.barrier()
        with contextlib.ExitStack() as st:
            pgb = load_row(D, D, "post_g", st)
            post_g = pgb.ap
            op_ = [sb([128, D], F32, "op", st) for _ in range(2)]
            xo = [sb([128, D], F32, "xo", st) for _ in range(2)]
            junk = sb([128, D], BF16, "junkE", st)
            ss2 = [sb([128, 4], F32, "ssE", st) for _ in range(2)]
            last = []
            for to in range(OT):
                o_b, x_b, ss = op_[to % 2], xo[to % 2], ss2[to % 2]
                Sc.dma("sp", [(o_b.ap, outp_d.ap[to])], [outp_d], [o_b], o_b)
                Sc.dma("sp", [(x_b.ap, x_loc[(OT0 + to) * 128:(OT0 + to + 1) * 128, :])], [], [x_b], x_b)
                Sc.op("act", lambda e: e.activation(out=junk.ap, in_=o_b.ap, func=AF.Square, accum_out=ss.ap[:, 0:1]), [o_b], [junk, ss])
                Sc.op("dve", lambda e: e.tensor_scalar(ss.ap[:, 1:2], ss.ap[:, 0:1], 1.0 / D, EPS, op0=ALU.mult, op1=ALU.add), [ss], [ss])
                Sc.op("act", lambda e: e.activation(out=ss.ap[:, 2:3], in_=ss.ap[:, 1:2], func=AF.Sqrt), [ss], [ss])
                Sc.op("dve", lambda e: e.reciprocal(ss.ap[:, 3:4], ss.ap[:, 2:3]), [ss], [ss])
                Sc.op("dve", lambda e: e.scalar_tensor_tensor(out=o_b.ap, in0=o_b.ap, scalar=ss.ap[:, 3:4], in1=post_g, op0=ALU.mult, op1=ALU.mult), [o_b, ss, pgb], [o_b])
                Sc.op("dve", lambda e: e.tensor_tensor(o_b.ap, o_b.ap, x_b.ap, ALU.add), [o_b, x_b], [o_b])
                last.append(Sc.dma("sp", [(out_d[to * 128:(to + 1) * 128, :], o_b.ap)], [o_b], [], o_b))
            Sc._wait("sp", last)
            Sc.barrier()
    return nc


def host_inputs(cfg, inputs):
    c = derive(cfg)
    D, S, T, NC_, SW, GN, SH, IH = c["D"], c["S"], c["T"], c["NCORE"], c["SW"], c["GN"], c["SH"], c["IH"]
    f32 = np.float32
    x = np.asarray(inputs["x"], f32)[0]
    p = np.arange(128)
    ident = np.eye(128, dtype=f32)
    pswA = np.zeros((128, 128), f32)
    pswA[(p + 64) % 128, p] = 1.0
    pswI = np.zeros((128, 128), f32)
    pswI[(p // 64) * 64 + ((p % 64) + 32) % 64, p] = 1.0
    tri = (p[:, None] <= p[None, :]).astype(f32)
    negm = np.where(p[:, None] <= p[None, :], 0.0, -30000.0).astype(f32)
    dmask = np.where(p[None, :] < ((p[:, None] // 64) + 1) * 64, 0.0, NEG).astype(f32)
    ones = np.ones((128, 128), f32)
    consts = np.ascontiguousarray(np.concatenate([ident, pswA, pswI, tri, negm, dmask, ones], 1))
    conv_w = np.asarray(inputs["conv_w"], f32)[0]
    conv_b = np.asarray(inputs["conv_b"], f32)[0]
    CH = conv_w.shape[1]
    convw = np.ascontiguousarray(conv_w.T.reshape(CH // 128, 128, 4).transpose(1, 0, 2))
    convb = np.ascontiguousarray(conv_b.reshape(CH // 128, 128).T)
    rows = np.concatenate([np.asarray(inputs[k], f32)[0] for k in
                           ("pre_norm_gain", "post_norm_gain", "ssd_norm_gain", "dt_bias", "a_log", "d_skip")])[None, :]
    rows = np.ascontiguousarray(rows)
    w_in = np.ascontiguousarray(np.asarray(inputs["w_in"], f32)[0])
    w_ba = np.ascontiguousarray(np.asarray(inputs["w_branch_attn"], f32)[0])
    w_bs = np.ascontiguousarray(np.asarray(inputs["w_branch_ssd"], f32)[0])
    w_o = np.ascontiguousarray(np.asarray(inputs["w_out"], f32)[0])
    invA = (1.0 / (10000.0 ** (np.arange(0, 128, 2, dtype=f32) / f32(128)))).astype(f32)
    invI = (1.0 / (10000.0 ** (np.arange(0, 64, 2, dtype=f32) / f32(64)))).astype(f32)
    maps = []
    for core in range(NC_):
        pad = S - (core + 1) * T
        x_loc = np.zeros((S, D), f32)
        x_loc[pad:] = x[0:(core + 1) * T]
        pos = np.maximum(np.arange(S) - pad, 0).astype(f32)
        angA = pos[:, None] * invA[None, :]
        angI = pos[:, None] * invI[None, :]
        cA, sA = np.cos(angA).astype(f32), np.sin(angA).astype(f32)
        cI, sI = np.cos(angI).astype(f32), np.sin(angI).astype(f32)
        ropeA = np.stack([np.concatenate([cA, cA], 1).T, np.concatenate([-sA, sA], 1).T]).astype(f32)
        ropeI = np.stack([np.concatenate([cI, cI, cI, cI], 1).T, np.concatenate([-sI, sI, -sI, sI], 1).T]).astype(f32)
        valid = (np.arange(S) >= pad)
        km = np.where(valid, 0.0, NEG).astype(f32)[None, :]
        vtm = np.ascontiguousarray(valid.astype(f32).reshape(S // 128, 128).T)
        maps.append(dict(x_loc=x_loc, w_in=w_in, consts=consts, ropeA=np.ascontiguousarray(ropeA),
                         ropeI=np.ascontiguousarray(ropeI), kmask=np.ascontiguousarray(km), valid_tm=vtm,
                         convw=convw, convb=convb, rows=rows, w_ba=w_ba, w_bs=w_bs, w_o=w_o))
    return maps


_NC_CACHE = {}


def run(cfg, inputs):
    key = tuple(sorted(cfg.items()))
    if key not in _NC_CACHE:
        _NC_CACHE[key] = build(cfg)
    nc = _NC_CACHE[key]
    maps = host_inputs(cfg, inputs)
    res = run_bass_kernel_spmd(nc, maps, core_ids=list(range(cfg["NCORE"])))
    out = np.concatenate([np.asarray(r["out"], np.float32) for r in res.results], 0)
    return out[None]


def kernel(**inputs):
    return run(CFG_FULL, inputs)
```

```python
import contextlib
import numpy as np
import concourse.bass as bass
import concourse.mybir as mybir
from concourse.bass_utils import run_bass_kernel_spmd

F32 = mybir.dt.float32
BF16 = mybir.dt.bfloat16
AF = mybir.ActivationFunctionType
ALU = mybir.AluOpType
AX = mybir.AxisListType

CFG_FULL = dict(D=4096, S=8192, NCORE=8, AW=2048, IH=32, SW=2048, G=8, TOPK=256)
NEG = -1.0e30


def derive(cfg):
    c = dict(cfg)
    c["T"] = c["S"] // c["NCORE"]
    c["KC"] = c["D"] // 128
    c["NH"] = c["AW"] // 128
    c["SH"] = c["SW"] // 64
    c["R"] = c["SH"] // c["G"]
    c["GN"] = c["G"] * 128
    AW, IH, SW, GN, SH, D = c["AW"], c["IH"], c["SW"], c["GN"], c["SH"], c["D"]
    offs = {}
    o = 0
    for name, w in (("q", AW), ("k", AW), ("v", AW), ("ga", AW), ("qi", IH * 64), ("ki", 64), ("wi", IH),
                    ("z", SW), ("xs", SW), ("B", GN), ("C", GN), ("dt", SH), ("mg", 2 * D)):
        offs[name] = o
        o += w
    c["offs"] = offs
    c["INW"] = o
    return c


class Buf:
    def __init__(self, ap, name):
        self.ap = ap
        self.name = name
        self.w = {}
        self.r = {}
        self.dsem = None
        self.dcnt = 0


class Sched:
    def __init__(self, nc, es):
        self.nc = nc
        self.es = es
        self.engs = {"pe": nc.tensor, "act": nc.scalar, "dve": nc.vector, "pool": nc.gpsimd, "sp": nc.sync}
        self.sem = {e: es.enter_context(nc.semaphore("sem_" + e)) for e in ("pe", "act", "dve", "pool")}
        self.cnt = {e: 0 for e in self.sem}
        self.seen = {e: {} for e in self.engs}
        self.pending_pe = False
        self.all_dma = {}

    def _wait(self, e, deps):
        best = {}
        for (sem, val, key) in deps:
            if key == "pe" and e == "pe":
                continue
            k = id(sem)
            if k not in best or best[k][1] < val:
                best[k] = (sem, val)
        for k, (sem, val) in best.items():
            if self.seen[e].get(k, 0) >= val:
                continue
            self.engs[e].wait_ge(sem, val)
            self.seen[e][k] = val

    @staticmethod
    def _merge(d, t):
        k = id(t[0])
        if k not in d or d[k][1] < t[1]:
            d[k] = t

    def _deps(self, reads, writes, disjoint):
        deps = []
        for b in reads:
            deps += list(b.w.values())
        for b in writes:
            if not disjoint:
                deps += list(b.w.values())
            deps += list(b.r.values())
        return deps

    def op(self, e, fn, reads=(), writes=(), inc=True):
        self._wait(e, self._deps(reads, writes, False))
        ins = fn(self.engs[e])
        if inc:
            self.cnt[e] += 1
            ins.then_inc(self.sem[e], 1)
            t = (self.sem[e], self.cnt[e], e)
        else:
            assert e == "pe"
            t = (self.sem[e], self.cnt[e] + 1, e)
        for b in reads:
            self._merge(b.r, t)
        for b in writes:
            self._merge(b.w, t)
        return t

    def dma(self, e, pairs, reads, writes, sb, disjoint=False):
        if sb.dsem is None:
            sb.dsem = self.es.enter_context(self.nc.semaphore("d_" + sb.name))
        self._wait(e, self._deps(reads, writes, disjoint))
        for (o, i) in pairs:
            ins = self.engs[e].dma_start(out=o, in_=i)
            sb.dcnt += 16
            ins.then_inc(sb.dsem, 16)
        t = (sb.dsem, sb.dcnt, "dma_" + sb.name)
        self.all_dma[id(sb.dsem)] = t
        for b in reads:
            self._merge(b.r, t)
        for b in writes:
            self._merge(b.w, t)
        return t

    def barrier(self):
        ts = [(self.sem[e], self.cnt[e], e) for e in self.sem if self.cnt[e] > 0] + list(self.all_dma.values())
        for e in self.engs:
            self._wait(e, [t for t in ts if not (t[2] == "pe" and e == "pe")])


def build(cfg):
    c = derive(cfg)
    D, S, T, KC, AW, NH, IH, SW, SH, G, R, GN, TOPK = (c[k] for k in
        ("D", "S", "T", "KC", "AW", "NH", "IH", "SW", "SH", "G", "R", "GN", "TOPK"))
    offs, INW = c["offs"], c["INW"]
    NT, NG = S // 128, S // 512
    OT, OG = T // 128, T // 512
    OT0, OG0 = NT - OT, NG - OG
    NQG = T // 256
    EPS = 1e-6
    nc = bass.Bass("TRN2", target_bir_lowering=False)

    def din(name, shape, dt=F32):
        return nc.dram_tensor(name, list(shape), dt, kind="ExternalInput").ap()

    def dscr(name, shape, dt):
        return Buf(nc.dram_tensor(name, list(shape), dt, kind="Internal").ap(), name)

    x_loc = din("x_loc", [S, D])
    w_in = din("w_in", [D, INW])
    consts = din("consts", [128, 7 * 128])
    ropeA = din("ropeA", [2, 128, S])
    ropeI = din("ropeI", [2, 128, S])
    kmask = din("kmask", [1, S])
    valid_tm = din("valid_tm", [128, NT])
    convw = din("convw", [128, (SW + 2 * GN) // 128, 4])
    convb = din("convb", [128, (SW + 2 * GN) // 128])
    rows = din("rows", [1, 2 * D + SW + 3 * SH])
    w_ba = din("w_ba", [AW, D])
    w_bs = din("w_bs", [SW, D])
    w_o = din("w_o", [D, D])
    out_d = nc.dram_tensor("out", [T, D], F32, kind="ExternalOutput").ap()

    hT_d = dscr("hT_d", [NG, 128, KC, 512], BF16)
    kT_d = dscr("kT_d", [NH, 128, S], BF16)
    V_d = dscr("V_d", [NT, 128, AW], BF16)
    kiT_d = dscr("kiT_d", [128, S], BF16)
    xs_d = dscr("xs_d", [NT, 128, SW], BF16)
    B_d = dscr("B_d", [NT, 128, GN], BF16)
    BT_d = dscr("BT_d", [G, 128, S], BF16)
    CT_d = dscr("CT_d", [G, 128, S], BF16)
    dt_d = dscr("dt_d", [NT, 128, SH], F32)
    qT_d = dscr("qT_d", [NH, 128, T], BF16)
    gaT_d = dscr("gaT_d", [NH, 128, T], BF16)
    qiT_d = dscr("qiT_d", [IH // 2, 128, T], BF16)
    wi_d = dscr("wi_d", [OT, 128, IH], F32)
    zs_d = dscr("zs_d", [OT, 128, SW], BF16)
    gates_d = dscr("gates_d", [OT, 128, 2 * D], BF16)
    yaT_d = dscr("yaT_d", [NH, 128, T], BF16)
    ybT_d = dscr("ybT_d", [OT, 128, SW], BF16)
    mT_d = dscr("mT_d", [OT, 128, KC * 128], BF16)
    outp_d = dscr("outp_d", [OT, 128, D], F32)

    with contextlib.ExitStack() as es:
        Sc = Sched(nc, es)
        uid = [0]

        def sb(shape, dt, name, stack=None):
            uid[0] += 1
            nm = f"{name}_{uid[0]}"
            return Buf(((stack or es).enter_context(nc.sbuf_tensor(nm, list(shape), dt)))[:], nm)

        def ps(shape, dt, name, stack=None):
            uid[0] += 1
            nm = f"{name}_{uid[0]}"
            return Buf(((stack or es).enter_context(nc.psum_tensor(nm, list(shape), dt)))[:], nm)

        cst = sb([128, 7 * 128], F32, "cst")
        Sc.dma("sp", [(cst.ap, consts)], [], [cst], cst)
        ident_f, pswA_f, pswI_f, tri_f, negm_f, dmask_f, ones_f = (cst.ap[:, i * 128:(i + 1) * 128] for i in range(7))
        cbf = sb([128, 4 * 128], BF16, "cbf")
        Sc.op("dve", lambda e: e.tensor_copy(cbf.ap[:, 0:384], cst.ap[:, 0:384]), [cst], [cbf])
        Sc.op("dve", lambda e: e.tensor_copy(cbf.ap[:, 384:512], ones_f), [cst], [cbf])
        ident_b, pswA_b, pswI_b, ones_b = (cbf.ap[:, i * 128:(i + 1) * 128] for i in range(4))
        NROW = 2 * D + SW + 3 * SH
        o_ = 2 * D + SW
        rowbc = sb([128, 3 * SH], F32, "rowbc")
        Sc.dma("sp", [(rowbc.ap, rows[:, o_:o_ + 3 * SH].partition_broadcast(128))], [], [rowbc], rowbc)
        dtb_bc = rowbc.ap[:, 0:SH]
        alog_bc = rowbc.ap[:, SH:2 * SH]
        dsk_bc = rowbc.ap[:, 2 * SH:3 * SH]

        def load_row(c0, n, name, stack):
            b = sb([128, n], F32, name, stack)
            Sc.dma("sp", [(b.ap, rows[:, c0:c0 + n].partition_broadcast(128))], [], [b], b)
            return b
        a_bc = sb([128, SH], F32, "a_bc")
        Sc.op("act", lambda e: e.activation(out=a_bc.ap, in_=alog_bc, func=AF.Exp), [rowbc], [a_bc])
        Sc.op("dve", lambda e: e.tensor_scalar(a_bc.ap, a_bc.ap, -1.0, None, op0=ALU.mult), [a_bc], [a_bc])
        vtm = sb([128, NT], F32, "vtm")
        Sc.dma("sp", [(vtm.ap, valid_tm)], [], [vtm], vtm)
        NCB = (SW + 2 * GN) // 128
        cw = sb([128, NCB, 4], F32, "cw")
        cbias = sb([128, NCB], F32, "cbias")
        Sc.dma("sp", [(cw.ap, convw)], [], [cw], cw)
        Sc.dma("sp", [(cbias.ap, convb)], [], [cbias], cbias)

        psb = [ps([128, 512], F32, f"psb{i}") for i in range(4)]
        pst = [ps([128, 1024], BF16, f"pst{i}") for i in range(2)]
        pctr = [0, 0]

        def nps():
            pctr[0] += 1
            return psb[pctr[0] % 4]

        def npt():
            pctr[1] += 1
            return pst[pctr[1] % 2]

        def copy_eng(i):
            return "act"

        def cp(e, out_ap, in_ap, reads, writes):
            if e == "act":
                return Sc.op("act", lambda q: q.activation(out=out_ap, in_=in_ap, func=AF.Copy), reads, writes)
            return Sc.op(e, lambda q: q.tensor_copy(out_ap, in_ap), reads, writes)

        with contextlib.ExitStack() as st:
            pre_gb = load_row(0, D, "pre_g", st)
            pre_g = pre_gb.ap
            xt = [sb([128, D], F32, "xt", st) for _ in range(2)]
            hb = [sb([128, D], BF16, "hb", st) for _ in range(2)]
            junk = sb([128, D], BF16, "junkA", st)
            hTt = [sb([128, KC, 512], BF16, "hTt", st) for _ in range(2)]
            ssb = [sb([128, 4], F32, "ssA", st) for _ in range(2)]
            for g in range(NG):
                hg = hTt[g % 2]
                for tt in range(4):
                    t = g * 4 + tt
                    xb, h_, ss = xt[t % 2], hb[t % 2], ssb[t % 2]
                    Sc.dma("sp", [(xb.ap, x_loc[t * 128:(t + 1) * 128, :])], [], [xb], xb)
                    Sc.op("act", lambda e: e.activation(out=junk.ap, in_=xb.ap, func=AF.Square, accum_out=ss.ap[:, 0:1]), [xb], [junk, ss])
                    Sc.op("dve", lambda e: e.tensor_scalar(ss.ap[:, 1:2], ss.ap[:, 0:1], 1.0 / D, EPS, op0=ALU.mult, op1=ALU.add), [ss], [ss])
                    Sc.op("act", lambda e: e.activation(out=ss.ap[:, 2:3], in_=ss.ap[:, 1:2], func=AF.Sqrt), [ss], [ss])
                    Sc.op("dve", lambda e: e.reciprocal(ss.ap[:, 3:4], ss.ap[:, 2:3]), [ss], [ss])
                    Sc.op("dve", lambda e: e.scalar_tensor_tensor(out=h_.ap, in0=xb.ap, scalar=ss.ap[:, 3:4], in1=pre_g, op0=ALU.mult, op1=ALU.mult), [xb, ss, pre_gb], [h_])
                    for k8 in range(0, KC, 8):
                        nk = min(8, KC - k8)
                        pt = npt()
                        for j in range(nk):
                            Sc.op("pe", lambda e, j=j: e.transpose(pt.ap[:, j * 128:(j + 1) * 128], h_.ap[:, (k8 + j) * 128:(k8 + j + 1) * 128], ident_b), [h_, cbf], [pt], inc=(j == nk - 1))
                        cp(copy_eng(k8 // 8), hg.ap[:, k8:k8 + nk, tt * 128:(tt + 1) * 128], pt.ap[:, 0:nk * 128].rearrange("p (k n) -> p k n", n=128), [pt], [hg])
                Sc.dma("sp", [(hT_d.ap[g], hg.ap)], [hg], [hT_d], hg, disjoint=True)
        Sc.barrier()

        STOP = cfg.get('STOP', 99)
        if STOP < 2:
            return nc
        with contextlib.ExitStack() as st:
            Wb = [sb([128, KC, 512], BF16, "Wb", st) for _ in range(2)]
            Hb = [sb([128, KC, 512], BF16, "Hb", st) for _ in range(2)]
            rtab = [sb([128, 4, 512], F32, "rtab", st) for _ in range(3)]
            stg = [sb([128, 4, 128], BF16, "stg", st) for _ in range(3)]
            ob = [sb([128, 512], BF16, "ob", st) for _ in range(3)]
            of = [sb([128, 512], F32, "of", st) for _ in range(4)]
            sm = [sb([128, 64], F32, "sm", st) for _ in range(2)]
            ctr = {"w": 0, "h": 0, "o": 0, "f": 0, "s": 0, "r": 0, "m": 0}

            def nxt(lst, key):
                ctr[key] += 1
                return lst[ctr[key] % len(lst)]

            pj = [0]
            DBG = cfg.get('DBG', 0)
            STOPB = cfg.get('STOPB', 999)

            def project(segs, groups, mode, cbk, need_rope=False):
                pj[0] += 1
                if pj[0] > STOPB:
                    return
                ncols = sum(n for _, n in segs)
                W = nxt(Wb, "w")
                pairs = []
                o = 0
                for (c0, n) in segs:
                    pairs.append((W.ap[:, :, o:o + n], w_in[:, c0:c0 + n].rearrange("(k p) c -> p k c", p=128)))
                    o += n
                Sc.dma("pool", pairs, [], [W], W)
                pend = []

                def load_g(g):
                    H = nxt(Hb, "h")
                    Sc.dma("sp", [(H.ap, hT_d.ap[g])], [hT_d], [H], H)
                    rt = None
                    if need_rope:
                        rt = nxt(rtab, "r")
                        Sc.dma("sp", [(rt.ap[:, 0:2, :], ropeA[:, :, g * 512:(g + 1) * 512].rearrange("a p s -> p a s")),
                                      (rt.ap[:, 2:4, :], ropeI[:, :, g * 512:(g + 1) * 512].rearrange("a p s -> p a s"))], [], [rt], rt)
                    return H, rt
                cur = load_g(groups[0])
                for gi, g in enumerate(groups):
                    H, rt = cur
                    if gi + 1 < len(groups):
                        cur = load_g(groups[gi + 1])
                    if mode == "fm":
                        for sbk in range((ncols + 127) // 128):
                            m = min(128, ncols - sbk * 128)
                            p = nps()
                            for k in range(KC):
                                Sc.op("pe", lambda e, k=k: e.matmul(p.ap[:m, :], W.ap[:, k, sbk * 128:sbk * 128 + m], H.ap[:, k, :], start=(k == 0), stop=(k == KC - 1)), [W, H], [p], inc=(k == KC - 1))
                            if pend:
                                pend.pop()()
                            pend.append(lambda sbk=sbk, m=m, g=g, gi=gi, p=p, rt=rt: cbk(sbk, m, g, gi, p, rt))
                    else:
                        for tt in range(4):
                            p = nps()
                            for k in range(KC):
                                Sc.op("pe", lambda e, k=k: e.matmul(p.ap[:, :ncols], H.ap[:, k, tt * 128:(tt + 1) * 128], W.ap[:, k, :ncols], start=(k == 0), stop=(k == KC - 1)), [W, H], [p], inc=(k == KC - 1))
                            if pend:
                                pend.pop()()
                            pend.append(lambda t_=g * 4 + tt, p=p: cbk(t_, ncols, p))
                if pend:
                    pend.pop()()

            def rope_store(p, m, rt, ti, psw_b, dst_buf, dst_ap):
                pf = nxt(of, "f")
                Sc.op("act", lambda e: e.activation(out=pf.ap[:m, :], in_=p.ap[:m, :], func=AF.Copy), [p], [pf])
                raw = nxt(ob, "o")
                Sc.op("act", lambda e: e.activation(out=raw.ap[:m, :], in_=p.ap[:m, :], func=AF.Copy), [p], [raw])
                p2 = nps()
                Sc.op("pe", lambda e: e.matmul(p2.ap[:m, :], psw_b[:m, :m], raw.ap[:m, :], start=True, stop=True), [raw, cbf], [p2])
                f3 = nxt(of, "f")
                Sc.op("act", lambda e: e.activation(out=f3.ap[:m, :], in_=p2.ap[:m, :], func=AF.Copy), [p2], [f3])
                Sc.op("dve", lambda e: e.tensor_tensor(pf.ap[:m, :], pf.ap[:m, :], rt.ap[:m, ti, :], ALU.mult), [pf, rt], [pf])
                Sc.op("dve", lambda e: e.tensor_tensor(f3.ap[:m, :], f3.ap[:m, :], rt.ap[:m, ti + 1, :], ALU.mult), [f3, rt], [f3])
                o = nxt(ob, "o")
                Sc.op("dve", lambda e: e.tensor_tensor(o.ap[:m, :], pf.ap[:m, :], f3.ap[:m, :], ALU.add), [pf, f3], [o])
                Sc.dma("sp", [(dst_ap, o.ap[:m, :])], [o], [dst_buf], o, disjoint=True)

            ALLG = list(range(NG))
            OWNG = list(range(OG0, NG))
            for cbk4 in range(AW // 512):
                def k_cb(sbk, m, g, gi, p, rt, cbk4=cbk4):
                    hd = cbk4 * 4 + sbk
                    rope_store(p, m, rt, 0, pswA_b, kT_d, kT_d.ap[hd][:, g * 512:(g + 1) * 512])
                project([(offs["k"] + cbk4 * 512, 512)], ALLG, "fm", k_cb, need_rope=True)
            for cbk4 in range(AW // 512):
                def v_cb(t, ncols, p, cbk4=cbk4):
                    o = nxt(ob, "o")
                    cp("act", o.ap, p.ap, [p], [o])
                    Sc.dma("sp", [(V_d.ap[t][:, cbk4 * 512:(cbk4 + 1) * 512], o.ap)], [o], [V_d], o, disjoint=True)
                project([(offs["v"] + cbk4 * 512, 512)], ALLG, "tm", v_cb)
            def ki_cb(sbk, m, g, gi, p, rt):
                rope_store(p, m, rt, 2, pswI_b, kiT_d, kiT_d.ap[:, g * 512:(g + 1) * 512])
            project([(offs["ki"], 64), (offs["ki"], 64)], ALLG, "fm", ki_cb, need_rope=True)

            conv_R = [[sb([128, 515], F32, "cR", st) for _ in range(2)] for _ in range(4)]

            def conv_block(col0, cblk0, nblk, groups, store):
                def cv_cb(sbk, m, g, gi, p, rt):
                    cbi = cblk0 + sbk
                    Rcur = conv_R[sbk][gi % 2]
                    Rprev = conv_R[sbk][(gi + 1) % 2]
                    if gi == 0:
                        Sc.op("pool", lambda e: e.memset(Rcur.ap[:, 0:3], 0.0), [], [Rcur])
                    else:
                        Sc.op("pool", lambda e: e.tensor_copy(Rcur.ap[:, 0:3], Rprev.ap[:, 512:515]), [Rprev], [Rcur])
                    Sc.op("act", lambda e: e.activation(out=Rcur.ap[:, 3:515], in_=p.ap, func=AF.Copy), [p], [Rcur])
                    acc = nxt(of, "f")
                    Sc.op("act", lambda e: e.activation(out=acc.ap, in_=Rcur.ap[:, 3:515], func=AF.Identity, bias=cbias.ap[:, cbi:cbi + 1], scale=cw.ap[:, cbi, 3:4]), [Rcur, cbias, cw], [acc])
                    for j in range(3):
                        Sc.op("dve", lambda e, j=j: e.scalar_tensor_tensor(out=acc.ap, in0=Rcur.ap[:, j:j + 512], scalar=cw.ap[:, cbi, j:j + 1], in1=acc.ap, op0=ALU.mult, op1=ALU.add), [Rcur, cw, acc], [acc])
                    o = nxt(ob, "o")
                    Sc.op("act", lambda e: e.activation(out=o.ap, in_=acc.ap, func=AF.Silu), [acc], [o])
                    store(sbk, g, gi, o)
                project([(col0, nblk * 128)], groups, "fm", cv_cb)

            def to_tm_store(o, g, dst_buf, dst_fn):
                pt = npt()
                for tt in range(4):
                    Sc.op("pe", lambda e, tt=tt: e.transpose(pt.ap[:, tt * 128:(tt + 1) * 128], o.ap[:, tt * 128:(tt + 1) * 128], ident_b), [o, cbf], [pt], inc=(tt == 3))
                s_ = nxt(stg, "s")
                cp("act", s_.ap, pt.ap[:, 0:512].rearrange("p (t n) -> p t n", n=128), [pt], [s_])
                Sc.dma("sp", [(dst_fn(g), s_.ap)], [s_], [dst_buf], s_, disjoint=True)

            if True:
                for cb4 in range(SW // 512):
                    def xs_store(sbk, g, gi, o, cb4=cb4):
                        c0 = cb4 * 512 + sbk * 128
                        to_tm_store(o, g, xs_d, lambda g: xs_d.ap[g * 4:(g + 1) * 4, :, c0:c0 + 128].rearrange("t p c -> p t c"))
                    conv_block(offs["xs"] + cb4 * 512, cb4 * 4, 4, ALLG, xs_store)
                for cb4 in range(GN // 512):
                    def b_store(sbk, g, gi, o, cb4=cb4):
                        gg = cb4 * 4 + sbk
                        Sc.dma("sp", [(BT_d.ap[gg][:, g * 512:(g + 1) * 512], o.ap)], [o], [BT_d], o, disjoint=True)
                        to_tm_store(o, g, B_d, lambda g: B_d.ap[g * 4:(g + 1) * 4, :, gg * 128:gg * 128 + 128].rearrange("t p c -> p t c"))
                    conv_block(offs["B"] + cb4 * 512, SW // 128 + cb4 * 4, min(4, GN // 128), ALLG, b_store)
                for cb4 in range(GN // 512):
                    def c_store(sbk, g, gi, o, cb4=cb4):
                        gg = cb4 * 4 + sbk
                        Sc.dma("sp", [(CT_d.ap[gg][:, g * 512:(g + 1) * 512], o.ap)], [o], [CT_d], o, disjoint=True)
                    conv_block(offs["C"] + cb4 * 512, (SW + GN) // 128 + cb4 * 4, min(4, GN // 128), [OG0 - 1] + OWNG, c_store)

            def dt_cb(t, ncols, p):
                s_ = nxt(sm, "m")
                Sc.op("act", lambda e: e.activation(out=s_.ap[:, 0:SH], in_=p.ap[:, 0:SH], func=AF.Copy), [p], [s_])
                Sc.op("dve", lambda e: e.tensor_tensor(s_.ap[:, 0:SH], s_.ap[:, 0:SH], dtb_bc, ALU.add), [s_, rowbc], [s_])
                Sc.op("act", lambda e: e.activation(out=s_.ap[:, 0:SH], in_=s_.ap[:, 0:SH], func=AF.Exp), [s_], [s_])
                Sc.op("act", lambda e: e.activation(out=s_.ap[:, 0:SH], in_=s_.ap[:, 0:SH], func=AF.Ln, bias=1.0), [s_], [s_])
                Sc.op("dve", lambda e: e.tensor_scalar(s_.ap[:, 0:SH], s_.ap[:, 0:SH], vtm.ap[:, t:t + 1], None, op0=ALU.mult), [s_, vtm], [s_])
                Sc.dma("sp", [(dt_d.ap[t], s_.ap[:, 0:SH])], [s_], [dt_d], s_, disjoint=True)
            project([(offs["dt"], SH)], ALLG, "tm", dt_cb)

            for cbk4 in range(AW // 512):
                def q_cb(sbk, m, g, gi, p, rt, cbk4=cbk4):
                    hd = cbk4 * 4 + sbk
                    rope_store(p, m, rt, 0, pswA_b, qT_d, qT_d.ap[hd][:, gi * 512:(gi + 1) * 512])
                project([(offs["q"] + cbk4 * 512, 512)], OWNG, "fm", q_cb, need_rope=True)
            for cbk4 in range(AW // 512):
                def ga_cb(sbk, m, g, gi, p, rt, cbk4=cbk4):
                    hd = cbk4 * 4 + sbk
                    o = nxt(ob, "o")
                    Sc.op("act", lambda e: e.activation(out=o.ap, in_=p.ap, func=AF.Silu), [p], [o])
                    Sc.dma("sp", [(gaT_d.ap[hd][:, gi * 512:(gi + 1) * 512], o.ap)], [o], [gaT_d], o, disjoint=True)
                project([(offs["ga"] + cbk4 * 512, 512)], OWNG, "fm", ga_cb)
            for cbk4 in range(IH * 64 // 512):
                def qi_cb(sbk, m, g, gi, p, rt, cbk4=cbk4):
                    hp = cbk4 * 4 + sbk
                    rope_store(p, m, rt, 2, pswI_b, qiT_d, qiT_d.ap[hp][:, gi * 512:(gi + 1) * 512])
                project([(offs["qi"] + cbk4 * 512, min(512, IH * 64))], OWNG, "fm", qi_cb, need_rope=True)
            wscale = float(IH ** -0.5 * 64 ** -0.5)

            def wi_cb(t, ncols, p):
                s_ = nxt(sm, "m")
                Sc.op("act", lambda e: e.activation(out=s_.ap[:, 0:IH], in_=p.ap[:, 0:IH], func=AF.Copy, scale=wscale), [p], [s_])
                Sc.dma("sp", [(wi_d.ap[t - OT0], s_.ap[:, 0:IH])], [s_], [wi_d], s_, disjoint=True)
            project([(offs["wi"], IH)], OWNG, "tm", wi_cb)
            for cb4 in range(SW // 512):
                def z_cb(t, ncols, p, cb4=cb4):
                    o = nxt(ob, "o")
                    Sc.op("act", lambda e: e.activation(out=o.ap, in_=p.ap, func=AF.Silu), [p], [o])
                    Sc.dma("sp", [(zs_d.ap[t - OT0][:, cb4 * 512:(cb4 + 1) * 512], o.ap)], [o], [zs_d], o, disjoint=True)
                project([(offs["z"] + cb4 * 512, 512)], OWNG, "tm", z_cb)
            for cb4 in range(2 * D // 512):
                def mg_cb(t, ncols, p, cb4=cb4):
                    o = nxt(ob, "o")
                    Sc.op("act", lambda e: e.activation(out=o.ap, in_=p.ap, func=AF.Sigmoid), [p], [o])
                    Sc.dma("sp", [(gates_d.ap[t - OT0][:, cb4 * 512:(cb4 + 1) * 512], o.ap)], [o], [gates_d], o, disjoint=True)
                project([(offs["mg"] + cb4 * 512, 512)], OWNG, "tm", mg_cb)
        Sc.barrier()

        if STOP < 3:
            return nc
        with contextlib.ExitStack() as st:
            ngb = load_row(2 * D, SW, "ngain", st)
            ngain = ngb.ap
            hst = sb([128, SH, 64], F32, "hst", st)
            Sc.op("pool", lambda e: e.memset(hst.ap, 0.0), [], [hst])
            xsb = [sb([128, SH, 64], BF16, "xsb", st) for _ in range(2)]
            Bb = [sb([128, GN], BF16, "Bb", st) for _ in range(2)]
            dtb = [sb([128, SH], F32, "dtb", st) for _ in range(2)]
            sc_ = [sb([128, 8, SH], F32, "sc", st) for _ in range(2)]
            xdtd = [sb([128, SH, 64], BF16, "xdtd", st) for _ in range(2)]
            xdt = sb([128, SH, 64], BF16, "xdt", st)
            hbf = sb([128, SH, 64], BF16, "hbf", st)
            Rm = sb([128, SH, 128], F32, "Rm", st)
            E4 = [sb([128, 4, 128], F32, "E4", st) for _ in range(2)]
            dec = [sb([128, 128], F32, "dec", st) for _ in range(2)]
            ea = [sb([128, 128], F32, "ea", st) for _ in range(2)]
            MT = [sb([128, 128], BF16, "MT", st) for _ in range(2)]
            CsT = [sb([128, 128], BF16, "CsT", st) for _ in range(2)]
            cbT = [sb([128, 128], F32, "cbT", st) for _ in range(G)]
            BTt = sb([128, G, 128], BF16, "BTt", st)
            CTt = sb([128, G, 128], BF16, "CTt", st)
            ysb = sb([128, SH, 64], F32, "ysb", st)
            tmpy = sb([128, 4, 64], F32, "tmpy", st)
            zsb = sb([128, SW], BF16, "zsb", st)
            ybn = sb([128, SW], BF16, "ybn", st)
            junkc = sb([128, SW // G], BF16, "junkc", st)
            ssq = sb([128, 3 * G], F32, "ssq", st)
            ybT = sb([128, SW // 128, 128], BF16, "ybTt", st)
            pSsb = [sb([128, 512], F32, "pSs", st) for _ in range(2)]
            def load_t(t):
                X, Bt, dtt = xsb[t % 2], Bb[t % 2], dtb[t % 2]
                Sc.dma("sp", [(X.ap.rearrange("p h d -> p (h d)"), xs_d.ap[t])], [xs_d], [X], X)
                Sc.dma("sp", [(Bt.ap, B_d.ap[t])], [B_d], [Bt], Bt)
                Sc.dma("sp", [(dtt.ap, dt_d.ap[t])], [dt_d], [dtt], dtt)
            load_t(0)
            for t in range(NT):
                X, Bt, dtt, s_ = xsb[t % 2], Bb[t % 2], dtb[t % 2], sc_[t % 2]
                if t + 1 < NT:
                    load_t(t + 1)
                dta, acum, dte, cd, wgt, nac, aend = (s_.ap[:, i, :] for i in range(7))
                Sc.op("dve", lambda e: e.tensor_tensor(dta, dtt.ap, a_bc.ap, ALU.mult), [dtt, a_bc], [s_])
                pa, pe_ = nps(), nps()
                Sc.op("pe", lambda e: e.matmul(pa.ap[:, 0:SH], tri_f, dta, start=True, stop=True), [cst, s_], [pa])
                Sc.op("pe", lambda e: e.matmul(pe_.ap[:, 0:SH], ones_f, dta, start=True, stop=True), [cst, s_], [pe_])
                Sc.op("act", lambda e: e.activation(out=acum, in_=pa.ap[:, 0:SH], func=AF.Copy), [pa], [s_])
                Sc.op("act", lambda e: e.activation(out=aend, in_=pe_.ap[:, 0:SH], func=AF.Copy), [pe_], [s_])
                Sc.op("dve", lambda e: e.tensor_tensor(dte, aend, acum, ALU.subtract), [s_], [s_])
                Sc.op("act", lambda e: e.activation(out=dte, in_=dte, func=AF.Exp), [s_], [s_])
                Sc.op("act", lambda e: e.activation(out=cd, in_=pe_.ap[:, 0:SH], func=AF.Exp), [pe_], [s_])
                Sc.op("dve", lambda e: e.tensor_tensor(wgt, dtt.ap, dte, ALU.mult), [dtt, s_], [s_])
                XD = xdtd[t % 2]
                Sc.op("pool", lambda e: e.tensor_tensor(XD.ap, X.ap, wgt.unsqueeze(2).broadcast_to([128, SH, 64]), ALU.mult), [X, s_], [XD])
                if t >= OT0:
                    to = t - OT0
                    Sc.op("dve", lambda e: e.tensor_scalar(nac, acum, -1.0, None, op0=ALU.mult), [s_], [s_])
                    Sc.op("pool", lambda e: e.tensor_tensor(xdt.ap, X.ap, dtt.ap.unsqueeze(2).broadcast_to([128, SH, 64]), ALU.mult), [X, dtt], [xdt])
                    Sc.op("act", lambda e: e.activation(out=hbf.ap, in_=hst.ap, func=AF.Copy), [hst], [hbf])
                    Sc.dma("sp", [(BTt.ap, BT_d.ap[:, :, t * 128:(t + 1) * 128].rearrange("g p s -> p g s"))], [BT_d], [BTt], BTt)
                    Sc.dma("sp", [(CTt.ap, CT_d.ap[:, :, t * 128:(t + 1) * 128].rearrange("g p s -> p g s"))], [CT_d], [CTt], CTt)
                    Sc.dma("sp", [(zsb.ap, zs_d.ap[to])], [zs_d], [zsb], zsb)
                    for g in range(G):
                        pc = nps()
                        Sc.op("pe", lambda e, g=g: e.matmul(pc.ap[:, 0:128], BTt.ap[:, g, :], CTt.ap[:, g, :], start=True, stop=True), [BTt, CTt], [pc])
                        cp("act", cbT[g].ap, pc.ap[:, 0:128], [pc], [cbT[g]])
                    Sc.op("pool", lambda e: e.tensor_tensor(Rm.ap, tri_f.unsqueeze(1).broadcast_to([128, SH, 128]), dta.unsqueeze(2).broadcast_to([128, SH, 128]), ALU.mult), [cst, s_], [Rm])
                    for hg in range(SH // 4):
                        pbc = nps()
                        Sc.op("pe", lambda e: e.matmul(pbc.ap, ones_f, Rm.ap[:, hg * 4:(hg + 1) * 4, :].rearrange("p h l -> p (h l)"), start=True, stop=True), [cst, Rm], [pbc])
                        e4 = E4[hg % 2]
                        Sc.op("act", lambda e: e.activation(out=e4.ap.rearrange("p h l -> p (h l)"), in_=pbc.ap, func=AF.Copy), [pbc], [e4])
                        Sc.op("dve", lambda e: e.tensor_tensor(e4.ap, e4.ap, negm_f.unsqueeze(1).broadcast_to([128, 4, 128]), ALU.add), [e4, cst], [e4])
                        py = nps()
                        for j in range(4):
                            hh = hg * 4 + j
                            g = hh // R
                            d_, ea_, mt_, cs_ = dec[j % 2], ea[j % 2], MT[j % 2], CsT[j % 2]
                            Sc.op("act", lambda e: e.activation(out=d_.ap, in_=e4.ap[:, j, :], func=AF.Exp, bias=nac[:, hh:hh + 1]), [e4, s_], [d_])
                            Sc.op("dve", lambda e: e.tensor_tensor(mt_.ap, cbT[g].ap, d_.ap, ALU.mult), [cbT[g], d_], [mt_])
                            Sc.op("act", lambda e: e.activation(out=ea_.ap, in_=pbc.ap[:, j * 128:(j + 1) * 128], func=AF.Exp), [pbc], [ea_])
                            Sc.op("dve", lambda e: e.tensor_tensor(cs_.ap, CTt.ap[:, g, :], ea_.ap, ALU.mult), [CTt, ea_], [cs_])
                            Sc.op("pe", lambda e: e.matmul(py.ap[:, j * 64:(j + 1) * 64], mt_.ap, xdt.ap[:, hh, :], start=True, stop=False), [mt_, xdt], [py], inc=False)
                            Sc.op("pe", lambda e: e.matmul(py.ap[:, j * 64:(j + 1) * 64], cs_.ap, hbf.ap[:, hh, :], start=False, stop=True), [cs_, hbf], [py])
                        Sc.op("dve", lambda e: e.tensor_tensor(tmpy.ap, X.ap[:, hg * 4:(hg + 1) * 4, :], dsk_bc[:, hg * 4:(hg + 1) * 4].unsqueeze(2).broadcast_to([128, 4, 64]), ALU.mult), [X, rowbc], [tmpy])
                        Sc.op("act", lambda e: e.activation(out=ysb.ap[:, hg * 4:(hg + 1) * 4, :].rearrange("p h d -> p (h d)"), in_=py.ap[:, 0:256], func=AF.Copy), [py], [ysb])
                        Sc.op("dve", lambda e: e.tensor_tensor(ysb.ap[:, hg * 4:(hg + 1) * 4, :], ysb.ap[:, hg * 4:(hg + 1) * 4, :], tmpy.ap, ALU.add), [ysb, tmpy], [ysb])
                    yflat = ysb.ap.rearrange("p h d -> p (h d)")
                    Sc.op("dve", lambda e: e.tensor_tensor(yflat, yflat, zsb.ap, ALU.mult), [ysb, zsb], [ysb])
                    GW = SW // G
                    for g in range(G):
                        Sc.op("act", lambda e, g=g: e.activation(out=junkc.ap, in_=yflat[:, g * GW:(g + 1) * GW], func=AF.Square, accum_out=ssq.ap[:, g:g + 1]), [ysb], [junkc, ssq])
                    Sc.op("dve", lambda e: e.tensor_scalar(ssq.ap[:, G:2 * G], ssq.ap[:, 0:G], 1.0 / GW, EPS, op0=ALU.mult, op1=ALU.add), [ssq], [ssq])
                    Sc.op("act", lambda e: e.activation(out=ssq.ap[:, G:2 * G], in_=ssq.ap[:, G:2 * G], func=AF.Sqrt), [ssq], [ssq])
                    Sc.op("dve", lambda e: e.reciprocal(ssq.ap[:, 2 * G:3 * G], ssq.ap[:, G:2 * G]), [ssq], [ssq])
                    for g in range(G):
                        Sc.op("dve", lambda e, g=g: e.scalar_tensor_tensor(out=ybn.ap[:, g * GW:(g + 1) * GW], in0=yflat[:, g * GW:(g + 1) * GW], scalar=ssq.ap[:, 2 * G + g:2 * G + g + 1], in1=ngain[:, g * GW:(g + 1) * GW], op0=ALU.mult, op1=ALU.mult), [ysb, ssq, ngb], [ybn])
                    for k8 in range(0, SW // 128, 8):
                        nk = min(8, SW // 128 - k8)
                        pt = npt()
                        for j in range(nk):
                            Sc.op("pe", lambda e, j=j: e.transpose(pt.ap[:, j * 128:(j + 1) * 128], ybn.ap[:, (k8 + j) * 128:(k8 + j + 1) * 128], ident_b), [ybn, cbf], [pt], inc=(j == nk - 1))
                        cp("act", ybT.ap[:, k8:k8 + nk, :], pt.ap[:, 0:nk * 128].rearrange("p (k n) -> p k n", n=128), [pt], [ybT])
                    Sc.dma("sp", [(ybT_d.ap[to], ybT.ap.rearrange("p k n -> p (k n)"))], [ybT], [ybT_d], ybT, disjoint=True)
                hfl = hst.ap.rearrange("p h d -> p (h d)")
                Sc.op("dve", lambda e: e.tensor_tensor(hst.ap, hst.ap, cd.unsqueeze(2).broadcast_to([128, SH, 64]), ALU.mult), [hst, s_], [hst])
                for b in range((SW + 511) // 512):
                    w = min(512, SW - b * 512)
                    pS = nps()
                    gpb = w // (R * 64)
                    for gg in range(gpb):
                        g = b * (512 // (R * 64)) + gg
                        Sc.op("pe", lambda e, g=g, gg=gg: e.matmul(pS.ap[:, gg * R * 64:(gg + 1) * R * 64], Bt.ap[:, g * 128:(g + 1) * 128], XD.ap[:, g * R:(g + 1) * R, :].rearrange("p h d -> p (h d)"), start=True, stop=True), [Bt, XD], [pS], inc=(gg == gpb - 1))
                    pSs = pSsb[b % 2]
                    Sc.op("act", lambda e: e.activation(out=pSs.ap[:, 0:w], in_=pS.ap[:, 0:w], func=AF.Copy), [pS], [pSs])
                    Sc.op("dve", lambda e: e.tensor_tensor(hfl[:, b * 512:b * 512 + w], hfl[:, b * 512:b * 512 + w], pSs.ap[:, 0:w], ALU.add), [hst, pSs], [hst])
        Sc.barrier()

        if STOP < 4:
            return nc
        with contextlib.ExitStack() as st:
            kiT = sb([128, S], BF16, "kiT", st)
            Sc.dma("sp", [(kiT.ap, kiT_d.ap)], [kiT_d], [kiT], kiT)
            kmb = sb([128, S], BF16, "kmb", st)
            Sc.dma("pool", [(kmb.ap, kmask.partition_broadcast(128))], [], [kmb], kmb)
            acc = sb([128, S], F32, "acc", st)
            mrm = sb([128, S], BF16, "mrm", st)
            maskT = sb([128, NT, 256], BF16, "maskT", st)
            qiT = sb([128, IH // 2, 128], BF16, "qiT", st)
            wit = sb([128, IH], F32, "wit", st)
            rl = [sb([128, 512], BF16, "rl", st) for _ in range(3)]
            bs = sb([128, 16], F32, "bs", st)
            kTh = [sb([128, S], BF16, "kTh", st) for _ in range(2)]
            Vh = [sb([128, NT, 128], BF16, "Vh", st) for _ in range(2)]
            qTh = [sb([128, 256], BF16, "qTh", st) for _ in range(2)]
            gah = [sb([128, 256], BF16, "gah", st) for _ in range(2)]
            pb = [sb([128, 2, 256], BF16, "pb", st) for _ in range(3)]
            pm = [sb([128, 2, 256], BF16, "pm", st) for _ in range(3)]
            rs = sb([128, 256], F32, "rs", st)
            of_ = sb([128, 256], F32, "ofin", st)
            ya = [sb([128, 256], BF16, "ya", st) for _ in range(2)]
            psos = ps([128, 256], F32, "psos", st)
            psss = ps([128, 256], F32, "psss", st)
            rc = [0]
            for qg in range(NQG):
                for j in range(2):
                    tq = OT0 + 2 * qg + j
                    to = 2 * qg + j
                    NKt = (tq + 1) * 128
                    Sc.dma("sp", [(qiT.ap, qiT_d.ap[:, :, to * 128:(to + 1) * 128].rearrange("h p s -> p h s"))], [qiT_d], [qiT], qiT)
                    Sc.dma("sp", [(wit.ap, wi_d.ap[to])], [wi_d], [wit], wit)
                    for kb in range((NKt + 511) // 512):
                        nk = min(512, NKt - kb * 512)
                        for hh in range(IH):
                            pr = hh % 2
                            p = nps()
                            Sc.op("pe", lambda e: e.matmul(p.ap[:, 0:nk], qiT.ap[pr * 64:(pr + 1) * 64, hh // 2, :], kiT.ap[pr * 64:(pr + 1) * 64, kb * 512:kb * 512 + nk], start=True, stop=True), [qiT, kiT], [p])
                            rc[0] += 1
                            r_ = rl[rc[0] % 3]
                            Sc.op("act", lambda e: e.activation(out=r_.ap[:, 0:nk], in_=p.ap[:, 0:nk], func=AF.Relu), [p], [r_])
                            a_ = acc.ap[:, kb * 512:kb * 512 + nk]
                            if hh == 0:
                                Sc.op("dve", lambda e: e.tensor_scalar(a_, r_.ap[:, 0:nk], wit.ap[:, 0:1], None, op0=ALU.mult), [r_, wit], [acc])
                            else:
                                Sc.op("dve", lambda e: e.scalar_tensor_tensor(out=a_, in0=r_.ap[:, 0:nk], scalar=wit.ap[:, hh:hh + 1], in1=a_, op0=ALU.mult, op1=ALU.add), [r_, wit, acc], [acc])
                    A = acc.ap[:, 0:NKt]
                    am, lo, hi, mid, cnt, ge, d1, d2 = (bs.ap[:, i:i + 1] for i in range(8))
                    Sc.op("dve", lambda e: e.tensor_reduce(am, A, AX.X, ALU.max, apply_absolute_value=True), [acc], [bs])
                    Sc.op("dve", lambda e: e.tensor_tensor(A, A, kmb.ap[:, 0:NKt], ALU.add), [acc, kmb], [acc])
                    Sc.op("dve", lambda e: e.tensor_tensor(acc.ap[:, NKt - 128:NKt], acc.ap[:, NKt - 128:NKt], dmask_f, ALU.add), [acc, cst], [acc])
                    Sc.op("dve", lambda e: e.tensor_scalar(hi, am, 1.0, None, op0=ALU.add), [bs], [bs])
                    Sc.op("dve", lambda e: e.tensor_scalar(lo, hi, -1.0, None, op0=ALU.mult), [bs], [bs])
                    for it in range(22):
                        Sc.op("dve", lambda e: e.tensor_tensor(mid, lo, hi, ALU.add), [bs], [bs])
                        Sc.op("dve", lambda e: e.tensor_scalar(mid, mid, 0.5, None, op0=ALU.mult), [bs], [bs])
                        Sc.op("dve", lambda e: e.tensor_scalar(mrm.ap[:, 0:NKt], A, mid, None, op0=ALU.is_ge, op1=ALU.add, accum_out=cnt), [acc, bs], [mrm, bs])
                        Sc.op("dve", lambda e: e.tensor_scalar(ge, cnt, float(TOPK), None, op0=ALU.is_ge), [bs], [bs])
                        Sc.op("dve", lambda e: e.tensor_tensor(d1, mid, lo, ALU.subtract), [bs], [bs])
                        Sc.op("dve", lambda e: e.tensor_tensor(d2, hi, mid, ALU.subtract), [bs], [bs])
                        Sc.op("dve", lambda e: e.scalar_tensor_tensor(out=lo, in0=d1, scalar=ge, in1=lo, op0=ALU.mult, op1=ALU.add), [bs], [bs])
                        Sc.op("dve", lambda e: e.scalar_tensor_tensor(out=hi, in0=d2, scalar=ge, in1=mid, op0=ALU.mult, op1=ALU.add), [bs], [bs])
                    Sc.op("dve", lambda e: e.tensor_scalar(mrm.ap[:, 0:NKt], A, lo, -30000.0, op0=ALU.is_lt, op1=ALU.mult), [acc, bs], [mrm])
                    nkb = NKt // 128
                    for k8 in range(0, nkb, 8):
                        n8 = min(8, nkb - k8)
                        pt = npt()
                        for jj in range(n8):
                            Sc.op("pe", lambda e, jj=jj: e.transpose(pt.ap[:, jj * 128:(jj + 1) * 128], mrm.ap[:, (k8 + jj) * 128:(k8 + jj + 1) * 128], ident_b), [mrm, cbf], [pt], inc=(jj == n8 - 1))
                        cp(copy_eng(k8 // 8), maskT.ap[:, k8:k8 + n8, j * 128:(j + 1) * 128], pt.ap[:, 0:n8 * 128].rearrange("p (k n) -> p k n", n=128), [pt], [maskT])
                    if j == 0:
                        Sc.op("pool", lambda e: e.memset(maskT.ap[:, nkb:nkb + 1, 0:128], -30000.0), [], [maskT])
                KBg = (OT0 + 2 * qg + 2)
                NKg = KBg * 128
                scale = 128.0 ** -0.5
                def load_hd(hd):
                    K_, V_, Q_, Ga = kTh[hd % 2], Vh[hd % 2], qTh[hd % 2], gah[hd % 2]
                    Sc.dma("sp", [(K_.ap[:, 0:NKg], kT_d.ap[hd][:, 0:NKg])], [kT_d], [K_], K_)
                    Sc.dma("sp", [(V_.ap[:, 0:KBg, :], V_d.ap[0:KBg, :, hd * 128:(hd + 1) * 128].rearrange("t p d -> p t d"))], [V_d], [V_], V_)
                    Sc.dma("sp", [(Q_.ap, qT_d.ap[hd][:, qg * 256:(qg + 1) * 256])], [qT_d], [Q_], Q_)
                    Sc.dma("sp", [(Ga.ap, gaT_d.ap[hd][:, qg * 256:(qg + 1) * 256])], [gaT_d], [Ga], Ga)
                load_hd(0)
                for hd in range(NH):
                    K_, V_, Q_, Ga = kTh[hd % 2], Vh[hd % 2], qTh[hd % 2], gah[hd % 2]
                    if hd + 1 < NH:
                        load_hd(hd + 1)
                    nk2 = KBg // 2

                    def qk(kb2):
                        pl = nps()
                        for j2 in range(2):
                            kb = kb2 * 2 + j2
                            Sc.op("pe", lambda e, j2=j2, kb=kb: e.matmul(pl.ap[:, j2 * 256:(j2 + 1) * 256], K_.ap[:, kb * 128:(kb + 1) * 128], Q_.ap, start=True, stop=False), [K_, Q_], [pl], inc=False)
                            Sc.op("pe", lambda e, j2=j2, kb=kb: e.matmul(pl.ap[:, j2 * 256:(j2 + 1) * 256], ident_b, maskT.ap[:, kb, :], start=False, stop=True), [cbf, maskT], [pl], inc=(j2 == 1))
                        return pl
                    LOOK = 2
                    pls = [qk(i) for i in range(min(LOOK, nk2))]
                    for kb2 in range(nk2):
                        pl = pls.pop(0)
                        if kb2 + LOOK < nk2:
                            pls.append(qk(kb2 + LOOK))
                        rc[0] += 1
                        p_, pm_ = pb[rc[0] % 3], pm[rc[0] % 3]
                        Sc.op("act", lambda e: e.activation(out=p_.ap.rearrange("p a q -> p (a q)"), in_=pl.ap, func=AF.Exp, scale=scale), [pl], [p_])
                        pm_ = p_
                        for j2 in range(2):
                            kb = kb2 * 2 + j2
                            first = (kb == 0)
                            last = (kb == KBg - 1)
                            Sc.op("pe", lambda e, j2=j2, kb=kb: e.matmul(psos.ap, V_.ap[:, kb, :], pm_.ap[:, j2, :], start=first, stop=last), [V_, pm_], [psos], inc=False)
                            Sc.op("pe", lambda e, j2=j2: e.matmul(psss.ap, ones_b, pm_.ap[:, j2, :], start=first, stop=last), [cbf, pm_], [psss], inc=(j2 == 1))
                    Sc.op("act", lambda e: e.activation(out=rs.ap, in_=psss.ap, func=AF.Copy), [psss], [rs])
                    Sc.op("dve", lambda e: e.reciprocal(rs.ap, rs.ap), [rs], [rs])
                    Sc.op("act", lambda e: e.activation(out=of_.ap, in_=psos.ap, func=AF.Copy), [psos], [of_])
                    Sc.op("dve", lambda e: e.tensor_tensor(of_.ap, of_.ap, rs.ap, ALU.mult), [of_, rs], [of_])
                    y_ = ya[hd % 2]
                    Sc.op("dve", lambda e: e.tensor_tensor(y_.ap, of_.ap, Ga.ap, ALU.mult), [of_, Ga], [y_])
                    Sc.dma("sp", [(yaT_d.ap[hd][:, qg * 256:(qg + 1) * 256], y_.ap)], [y_], [yaT_d], y_, disjoint=True)
        Sc.barrier()

        if STOP < 5:
            return nc
        with contextlib.ExitStack() as st:
            NKS = SW // 128
            yaT = sb([128, NH, T], BF16, "yaTs", st)
            ybT2 = sb([128, OT, NKS, 128], BF16, "ybTs", st)
            Sc.dma("sp", [(yaT.ap, yaT_d.ap.rearrange("h p s -> p h s"))], [yaT_d], [yaT], yaT)
            Sc.dma("sp", [(ybT2.ap, ybT_d.ap.rearrange("t p (k n) -> p t k n", n=128))], [ybT_d], [ybT2], ybT2)
            Wa = [sb([128, NH, 512], BF16, "Wa", st) for _ in range(2)]
            Ws = [sb([128, NKS, 512], BF16, "Ws", st) for _ in range(2)]
            gt = [sb([128, 2, 512], BF16, "gt", st) for _ in range(2)]
            m1 = [sb([128, 512], F32, "m1", st) for _ in range(2)]
            m2 = [sb([128, 512], F32, "m2", st) for _ in range(2)]
            mg = [sb([128, 512], BF16, "mgd", st) for _ in range(2)]
            mTs = [sb([128, 4, 128], BF16, "mTs", st) for _ in range(2)]
            it_ = 0
            for cb in range(D // 512):
                wa, ws = Wa[cb % 2], Ws[cb % 2]
                Sc.dma("pool", [(wa.ap, w_ba[:, cb * 512:(cb + 1) * 512].rearrange("(k p) c -> p k c", p=128))], [], [wa], wa)
                Sc.dma("pool", [(ws.ap, w_bs[:, cb * 512:(cb + 1) * 512].rearrange("(k p) c -> p k c", p=128))], [], [ws], ws)
                for to in range(OT):
                    it_ += 1
                    g_, a1, a2, mm_, mt_ = gt[it_ % 2], m1[it_ % 2], m2[it_ % 2], mg[it_ % 2], mTs[it_ % 2]
                    Sc.dma("sp", [(g_.ap, gates_d.ap[to].rearrange("p (a c) -> p a c", a=2)[:, :, cb * 512:(cb + 1) * 512])], [gates_d], [g_], g_)
                    pA, pB = nps(), nps()
                    for k in range(NH):
                        Sc.op("pe", lambda e, k=k: e.matmul(pA.ap, yaT.ap[:, k, to * 128:(to + 1) * 128], wa.ap[:, k, :], start=(k == 0), stop=(k == NH - 1)), [yaT, wa], [pA], inc=(k == NH - 1))
                    for k in range(NKS):
                        Sc.op("pe", lambda e, k=k: e.matmul(pB.ap, ybT2.ap[:, to, k, :], ws.ap[:, k, :], start=(k == 0), stop=(k == NKS - 1)), [ybT2, ws], [pB], inc=(k == NKS - 1))
                    Sc.op("act", lambda e: e.activation(out=a1.ap, in_=pA.ap, func=AF.Copy), [pA], [a1])
                    Sc.op("dve", lambda e: e.tensor_tensor(a1.ap, a1.ap, g_.ap[:, 0, :], ALU.mult), [a1, g_], [a1])
                    Sc.op("act", lambda e: e.activation(out=a2.ap, in_=pB.ap, func=AF.Copy), [pB], [a2])
                    Sc.op("dve", lambda e: e.tensor_tensor(a2.ap, a2.ap, g_.ap[:, 1, :], ALU.mult), [a2, g_], [a2])
                    Sc.op("pool", lambda e: e.tensor_tensor(mm_.ap, a1.ap, a2.ap, ALU.add), [a1, a2], [mm_])
                    pt = npt()
                    for jj in range(4):
                        Sc.op("pe", lambda e, jj=jj: e.transpose(pt.ap[:, jj * 128:(jj + 1) * 128], mm_.ap[:, jj * 128:(jj + 1) * 128], ident_b), [mm_, cbf], [pt], inc=(jj == 3))
                    cp("act", mt_.ap, pt.ap[:, 0:512].rearrange("p (k n) -> p k n", n=128), [pt], [mt_])
                    Sc.dma("sp", [(mT_d.ap[to][:, cb * 512:(cb + 1) * 512], mt_.ap.rearrange("p k n -> p (k n)"))], [mt_], [mT_d], mt_, disjoint=True)
        Sc.barrier()
        with contextlib.ExitStack() as st:
            mTa = sb([128, OT, KC, 128], BF16, "mTa", st)
            Sc.dma("sp", [(mTa.ap, mT_d.ap.rearrange("t p (k n) -> p t k n", n=128))], [mT_d], [mTa], mTa)
            Wo = [sb([128, KC, 512], BF16, "Wo", st) for _ in range(2)]
            oo = [sb([128, 512], F32, "oo", st) for _ in range(2)]
            it_ = 0
            for cb in range(D // 512):
                wo = Wo[cb % 2]
                Sc.dma("pool", [(wo.ap, w_o[:, cb * 512:(cb + 1) * 512].rearrange("(k p) c -> p k c", p=128))], [], [wo], wo)
                for to in range(OT):
                    it_ += 1
                    o_b = oo[it_ % 2]
                    p = nps()
                    for k in range(KC):
                        Sc.op("pe", lambda e, k=k: e.matmul(p.ap, mTa.ap[:, to, k, :], wo.ap[:, k, :], start=(k == 0), stop=(k == KC - 1)), [mTa, wo], [p], inc=(k == KC - 1))
                    cp("act", o_b.ap, p.ap, [p], [o_b])
                    Sc.dma("sp", [(outp_d.ap[to][:, cb * 512:(cb + 1) * 512], o_b.ap)], [o_b], [outp_d], o_b, disjoint=True)
        Sc.barrier()
        with contextlib.ExitStack() as st:
            pgb = load_row(D, D, "post_g", st)
            post_g = pgb.ap
            op_ = [sb([128, D], F32, "op", st) for _ in range(2)]
            xo = [sb([128, D], F32, "xo", st) for _ in range(2)]
            junk = sb([128, D], BF16, "junkE", st)
            ss2 = [sb([128, 4], F32, "ssE", st) for _ in range(2)]
            last = []
            for to in range(OT):
                o_b, x_b, ss = op_[to % 2], xo[to % 2], ss2[to % 2]
                Sc.dma("sp", [(o_b.ap, outp_d.ap[to])], [outp_d], [o_b], o_b)
                Sc.dma("sp", [(x_b.ap, x_loc[(OT0 + to) * 128:(OT0 + to + 1) * 128, :])], [], [x_b], x_b)
                Sc.op("act", lambda e: e.activation(out=junk.ap, in_=o_b.ap, func=AF.Square, accum_out=ss.ap[:, 0:1]), [o_b], [junk, ss])
                Sc.op("dve", lambda e: e.tensor_scalar(ss.ap[:, 1:2], ss.ap[:, 0:1], 1.0 / D, EPS, op0=ALU.mult, op1=ALU.add), [ss], [ss])
                Sc.op("act", lambda e: e.activation(out=ss.ap[:, 2:3], in_=ss.ap[:, 1:2], func=AF.Sqrt), [ss], [ss])
                Sc.op("dve", lambda e: e.reciprocal(ss.ap[:, 3:4], ss.ap[:, 2:3]), [ss], [ss])
                Sc.op("dve", lambda e: e.scalar_tensor_tensor(out=o_b.ap, in0=o_b.ap, scalar=ss.ap[:, 3:4], in1=post_g, op0=ALU.mult, op1=ALU.mult), [o_b, ss, pgb], [o_b])
                Sc.op("dve", lambda e: e.tensor_tensor(o_b.ap, o_b.ap, x_b.ap, ALU.add), [o_b, x_b], [o_b])
                last.append(Sc.dma("sp", [(out_d[to * 128:(to + 1) * 128, :], o_b.ap)], [o_b], [], o_b))
            Sc._wait("sp", last)
            Sc.barrier()
    return nc


def host_inputs(cfg, inputs):
    c = derive(cfg)
    D, S, T, NC_, SW, GN, SH, IH = c["D"], c["S"], c["T"], c["NCORE"], c["SW"], c["GN"], c["SH"], c["IH"]
    f32 = np.float32
    x = np.asarray(inputs["x"], f32)[0]
    p = np.arange(128)
    ident = np.eye(128, dtype=f32)
    pswA = np.zeros((128, 128), f32)
    pswA[(p + 64) % 128, p] = 1.0
    pswI = np.zeros((128, 128), f32)
    pswI[(p // 64) * 64 + ((p % 64) + 32) % 64, p] = 1.0
    tri = (p[:, None] <= p[None, :]).astype(f32)
    negm = np.where(p[:, None] <= p[None, :], 0.0, -30000.0).astype(f32)
    dmask = np.where(p[None, :] < ((p[:, None] // 64) + 1) * 64, 0.0, NEG).astype(f32)
    ones = np.ones((128, 128), f32)
    consts = np.ascontiguousarray(np.concatenate([ident, pswA, pswI, tri, negm, dmask, ones], 1))
    conv_w = np.asarray(inputs["conv_w"], f32)[0]
    conv_b = np.asarray(inputs["conv_b"], f32)[0]
    CH = conv_w.shape[1]
    convw = np.ascontiguousarray(conv_w.T.reshape(CH // 128, 128, 4).transpose(1, 0, 2))
    convb = np.ascontiguousarray(conv_b.reshape(CH // 128, 128).T)
    rows = np.concatenate([np.asarray(inputs[k], f32)[0] for k in
                           ("pre_norm_gain", "post_norm_gain", "ssd_norm_gain", "dt_bias", "a_log", "d_skip")])[None, :]
    rows = np.ascontiguousarray(rows)
    w_in = np.ascontiguousarray(np.asarray(inputs["w_in"], f32)[0])
    w_ba = np.ascontiguousarray(np.asarray(inputs["w_branch_attn"], f32)[0])
    w_bs = np.ascontiguousarray(np.asarray(inputs["w_branch_ssd"], f32)[0])
    w_o = np.ascontiguousarray(np.asarray(inputs["w_out"], f32)[0])
    invA = (1.0 / (10000.0 ** (np.arange(0, 128, 2, dtype=f32) / f32(128)))).astype(f32)
    invI = (1.0 / (10000.0 ** (np.arange(0, 64, 2, dtype=f32) / f32(64)))).astype(f32)
    maps = []
    for core in range(NC_):
        pad = S - (core + 1) * T
        x_loc = np.zeros((S, D), f32)
        x_loc[pad:] = x[0:(core + 1) * T]
        pos = np.maximum(np.arange(S) - pad, 0).astype(f32)
        angA = pos[:, None] * invA[None, :]
        angI = pos[:, None] * invI[None, :]
        cA, sA = np.cos(angA).astype(f32), np.sin(angA).astype(f32)
        cI, sI = np.cos(angI).astype(f32), np.sin(angI).astype(f32)
        ropeA = np.stack([np.concatenate([cA, cA], 1).T, np.concatenate([-sA, sA], 1).T]).astype(f32)
        ropeI = np.stack([np.concatenate([cI, cI, cI, cI], 1).T, np.concatenate([-sI, sI, -sI, sI], 1).T]).astype(f32)
        valid = (np.arange(S) >= pad)
        km = np.where(valid, 0.0, NEG).astype(f32)[None, :]
        vtm = np.ascontiguousarray(valid.astype(f32).reshape(S // 128, 128).T)
        maps.append(dict(x_loc=x_loc, w_in=w_in, consts=consts, ropeA=np.ascontiguousarray(ropeA),
                         ropeI=np.ascontiguousarray(ropeI), kmask=np.ascontiguousarray(km), valid_tm=vtm,
                         convw=convw, convb=convb, rows=rows, w_ba=w_ba, w_bs=w_bs, w_o=w_o))
    return maps


_NC_CACHE = {}


def run(cfg, inputs):
    key = tuple(sorted(cfg.items()))
    if key not in _NC_CACHE:
        _NC_CACHE[key] = build(cfg)
    nc = _NC_CACHE[key]
    maps = host_inputs(cfg, inputs)
    res = run_bass_kernel_spmd(nc, maps, core_ids=list(range(cfg["NCORE"])))
    out = np.concatenate([np.asarray(r["out"], np.float32) for r in res.results], 0)
    return out[None]


def kernel(**inputs):
    return run(CFG_FULL, inputs)
```
